# Optimizing a Trainium2 kernel written in Bass

```python
import math
import jax
import jax.numpy as jnp
from jax import lax
import numpy as np

D_MODEL = 2048
BATCH = 4
SEQ = 4096
DEPTH = 2

CTX_LEN = 256
GRID_W = 64
N_BRANCH = 4
BRANCH_W = 1024
ROPE_DIM = 64
ROPE_BASE = 10000.0
Q_BLOCK = 128

MLA_HEADS = 8
MLA_NOPE = 128
MLA_ROPE = ROPE_DIM
MLA_V = 128
MLA_Q_LORA = 512
MLA_KV_LORA = 256

S5_GROUP = 16
S5_GROUPS = BRANCH_W // S5_GROUP
S5_STATE = 64
S5_DT_MIN = 0.001
S5_DT_MAX = 0.1

SSD_HEAD_DIM = 64
SSD_HEADS = BRANCH_W // SSD_HEAD_DIM
SSD_GROUPS = 4
SSD_STATE = 128
SSD_CONV = 3
SSD_CHUNK = 128
SSD_CONV_CH = BRANCH_W + 2 * SSD_GROUPS * SSD_STATE
SSD_DT_MIN = 0.001
SSD_DT_MAX = 0.1

DIFF_HEAD_DIM = ROPE_DIM
DIFF_HEADS = BRANCH_W // (2 * DIFF_HEAD_DIM)

LN_EPS = 1e-5
RMS_EPS = 1e-6
DEEPNORM_ALPHA = (2 * DEPTH) ** 0.25
DEEPNORM_BETA = (8 * DEPTH) ** -0.25

IN_SPLITS = (
    MLA_Q_LORA, MLA_KV_LORA, MLA_ROPE, BRANCH_W,
    BRANCH_W, BRANCH_W,
    BRANCH_W, SSD_CONV_CH, SSD_HEADS,
    BRANCH_W, BRANCH_W, BRANCH_W, BRANCH_W,
    N_BRANCH * D_MODEL,
)
SPLIT_POINTS = tuple(sum(IN_SPLITS[: i + 1]) for i in range(len(IN_SPLITS) - 1))
N_IN = sum(IN_SPLITS)

kernel_name = 'hybrid_mla_s5_ssd_diffattn_flow_block'


def layer_norm(x):
    x32 = x.astype(jnp.float32)
    mu = jnp.mean(x32, -1, keepdims=True)
    var = jnp.mean(jnp.square(x32 - mu), -1, keepdims=True)
    return ((x32 - mu) * lax.rsqrt(var + LN_EPS)).astype(x.dtype)


def rms_norm(x, g):
    x32 = x.astype(jnp.float32)
    y = x32 * lax.rsqrt(jnp.mean(jnp.square(x32), -1, keepdims=True) + RMS_EPS)
    return y.astype(x.dtype) * g


def maybe_flip(t, flip):
    return t[:, ::-1] if flip else t


def flatten_heads(y):
    return y.reshape(y.shape[0], y.shape[1], -1)


def rope_tables(row_id, col_id, dim):
    quarter = dim // 4
    inv_freq = ROPE_BASE ** (-jnp.arange(quarter, dtype=jnp.float32) / quarter)
    ang = jnp.concatenate([row_id[:, None] * inv_freq, col_id[:, None] * inv_freq], axis=-1)
    return jnp.cos(ang), jnp.sin(ang)


def apply_rope(x, cos, sin):
    xp = x.reshape(*x.shape[:-1], -1, 2)
    xr, xi = xp[..., 0], xp[..., 1]
    c = cos[:, None, :].astype(x.dtype)
    s = sin[:, None, :].astype(x.dtype)
    return jnp.stack([xr * c - xi * s, xr * s + xi * c], axis=-1).reshape(x.shape)


def sweep_query_blocks(fn, q):
    b, t = q.shape[:2]
    nb = t // Q_BLOCK
    blocks = jnp.moveaxis(q.reshape(b, nb, Q_BLOCK, *q.shape[2:]), 1, 0)
    out = lax.map(fn, blocks)
    return jnp.moveaxis(out, 0, 1).reshape(b, t, *out.shape[3:])


def softmax_attend(q, k, v, scale):
    s = jnp.einsum('bqhd,bkhd->bhqk', q, k).astype(jnp.float32) * scale
    p = jax.nn.softmax(s, axis=-1).astype(v.dtype)
    return jnp.einsum('bhqk,bkhd->bqhd', p, v)


def mla_queries(cq, q_norm, w_uq, rope):
    b, t = cq.shape[:2]
    q = (rms_norm(cq, q_norm) @ w_uq).reshape(b, t, MLA_HEADS, MLA_NOPE + MLA_ROPE)
    if rope is None:
        return q
    return jnp.concatenate([q[..., :MLA_NOPE], apply_rope(q[..., MLA_NOPE:], *rope)], axis=-1)


def mla_keys_values(ckv, kr, kv_norm, w_ukv, rope):
    b, t = ckv.shape[:2]
    kv = (rms_norm(ckv, kv_norm) @ w_ukv).reshape(b, t, MLA_HEADS, MLA_NOPE + MLA_V)
    k_rope = kr[:, :, None, :]
    if rope is not None:
        k_rope = apply_rope(k_rope, *rope)
    k_rope = jnp.broadcast_to(k_rope, (b, t, MLA_HEADS, MLA_ROPE))
    k = jnp.concatenate([kv[..., :MLA_NOPE], k_rope], axis=-1)
    return k, kv[..., MLA_NOPE:]


def mla_mixer(lat, ctx, q_norm, w_uq, kv_norm, w_ukv, rope, ctx_out):
    cq_l, ckv_l, kr_l = lat
    cq_c, ckv_c, kr_c = ctx
    scale = (MLA_NOPE + MLA_ROPE) ** -0.5
    k_c, v_c = mla_keys_values(ckv_c, kr_c, kv_norm, w_ukv, None)
    k_l, v_l = mla_keys_values(ckv_l, kr_l, kv_norm, w_ukv, rope)
    k_all = jnp.concatenate([k_l, k_c], axis=1)
    v_all = jnp.concatenate([v_l, v_c], axis=1)
    q_l = mla_queries(cq_l, q_norm, w_uq, rope)
    y_l = sweep_query_blocks(lambda qb: softmax_attend(qb, k_all, v_all, scale), q_l)
    if not ctx_out:
        return flatten_heads(y_l), None
    y_c = softmax_attend(mla_queries(cq_c, q_norm, w_uq, None), k_c, v_c, scale)
    return flatten_heads(y_l), flatten_heads(y_c)


def s5_discretise(lam_re, lam_im, log_dt, b_re, b_im):
    lr = lam_re.astype(jnp.float32)
    li = lam_im.astype(jnp.float32)
    dt = jnp.exp(log_dt.astype(jnp.float32))[:, None]
    mag = jnp.exp(lr * dt)
    a_re = mag * jnp.cos(li * dt)
    a_im = mag * jnp.sin(li * dt)
    den = lr * lr + li * li
    f_re = ((a_re - 1.0) * lr + a_im * li) / den
    f_im = (a_im * lr - (a_re - 1.0) * li) / den
    br = b_re.astype(jnp.float32)
    bi = b_im.astype(jnp.float32)
    bb_re = f_re[..., None] * br - f_im[..., None] * bi
    bb_im = f_re[..., None] * bi + f_im[..., None] * br
    return a_re, a_im, bb_re, bb_im


def complex_affine_combine(e1, e2):
    a1r, a1i, b1r, b1i = e1
    a2r, a2i, b2r, b2i = e2
    return (a2r * a1r - a2i * a1i, a2r * a1i + a2i * a1r,
            a2r * b1r - a2i * b1i + b2r, a2r * b1i + a2i * b1r + b2i)


def s5_states(u, h0_re, h0_im, a_re, a_im, bb_re, bb_im):
    u32 = u.astype(jnp.float32)
    bu_re = jnp.einsum('blgs,gps->blgp', u32, bb_re)
    bu_im = jnp.einsum('blgs,gps->blgp', u32, bb_im)
    bu_re = bu_re.at[:, 0].add(a_re * h0_re - a_im * h0_im)
    bu_im = bu_im.at[:, 0].add(a_re * h0_im + a_im * h0_re)
    n = u.shape[1]
    a_re_seq = jnp.broadcast_to(a_re, (1, n) + a_re.shape)
    a_im_seq = jnp.broadcast_to(a_im, (1, n) + a_im.shape)
    _, _, h_re, h_im = lax.associative_scan(
        complex_affine_combine, (a_re_seq, a_im_seq, bu_re, bu_im), axis=1)
    return h_re, h_im


def s5_readout(h_re, h_im, c_re, c_im):
    return (jnp.einsum('blgp,gsp->blgs', h_re, c_re.astype(jnp.float32))
            - jnp.einsum('blgp,gsp->blgs', h_im, c_im.astype(jnp.float32)))


def s5_glu(y, w_glu, b_glu):
    g = jax.nn.gelu(y)
    return g * jax.nn.sigmoid(g @ w_glu + b_glu)


def s5_mixer(u_l, u_c, lam_re, lam_im, log_dt, b_re, b_im, c_re, c_im, d, w_glu, b_glu, ctx_out):
    b = u_l.shape[0]
    ul = u_l.reshape(b, u_l.shape[1], S5_GROUPS, S5_GROUP)
    uc = u_c.reshape(b, u_c.shape[1], S5_GROUPS, S5_GROUP)
    d_g = d.astype(jnp.float32).reshape(S5_GROUPS, S5_GROUP)
    zero = jnp.zeros((b, S5_GROUPS, S5_STATE), jnp.float32)
    y_l = ul.astype(jnp.float32) * d_g
    y_c = uc.astype(jnp.float32) * d_g if ctx_out else None
    for direction in range(2):
        flip = direction == 1
        disc = s5_discretise(lam_re[direction], lam_im[direction], log_dt[direction],
                             b_re[direction], b_im[direction])
        cr, ci = c_re[direction], c_im[direction]
        hc_re, hc_im = s5_states(maybe_flip(uc, flip), zero, zero, *disc)
        hl_re, hl_im = s5_states(maybe_flip(ul, flip), hc_re[:, -1], hc_im[:, -1], *disc)
        y_l = y_l + maybe_flip(s5_readout(hl_re, hl_im, cr, ci), flip)
        if ctx_out:
            y_c = y_c + maybe_flip(s5_readout(hc_re, hc_im, cr, ci), flip)

    def finish(y, u):
        return s5_glu(y.reshape(u.shape).astype(u.dtype), w_glu, b_glu)

    return finish(y_l, u_l), (finish(y_c, u_c) if ctx_out else None)


def depthwise_conv_centred(x, w, bias):
    k, ch = w.shape
    y = lax.conv_general_dilated(
        x, w[:, None, :], window_strides=(1,), padding=[((k - 1) // 2, k // 2)],
        dimension_numbers=('NWC', 'WIO', 'NWC'), feature_group_count=ch)
    return y + bias


def ssd_chunked(x, dt, a, bm, cm, h0, with_output):
    b, n, nh, hp = x.shape
    ng, ns = bm.shape[2:]
    r = nh // ng
    q = SSD_CHUNK
    nc = n // q
    da = jnp.transpose((dt * a).reshape(b, nc, q, ng, r), (0, 3, 4, 1, 2))
    cum = jnp.cumsum(da, axis=-1)
    xdt = (x * dt[..., None]).reshape(b, nc, q, ng, r, hp)
    bc = bm.reshape(b, nc, q, ng, ns)
    cc = cm.reshape(b, nc, q, ng, ns)
    to_end = jnp.exp(cum[..., -1:] - cum)
    states = jnp.einsum('bcjgn,bgrcj,bcjgrp->bcgrpn', bc, to_end, xdt)
    chunk_decay = jnp.exp(cum[..., -1])

    def step(h, inp):
        dec, st = inp
        return dec[..., None, None] * h + st, h

    h_last, h_in = lax.scan(step, h0.reshape(b, ng, r, hp, ns),
                            (jnp.moveaxis(chunk_decay, -1, 0), jnp.moveaxis(states, 1, 0)))
    h_last = h_last.reshape(b, nh, hp, ns)
    if not with_output:
        return None, h_last
    h_in = jnp.moveaxis(h_in, 0, 1)
    lower = jnp.tril(jnp.ones((q, q), dtype=bool))
    decay = jnp.exp(jnp.where(lower, cum[..., :, None] - cum[..., None, :], -jnp.inf))
    cb = jnp.einsum('bcign,bcjgn->bgcij', cc, bc)
    y_diag = jnp.einsum('bgcij,bgrcij,bcjgrp->bcigrp', cb, decay, xdt)
    y_off = jnp.einsum('bcign,bcgrpn,bgrci->bcigrp', cc, h_in, jnp.exp(cum))
    return (y_diag + y_off).reshape(b, n, nh, hp), h_last


def ssd_prepare(xbc, dt_raw, conv_w, conv_b, dt_bias):
    b, n = xbc.shape[:2]
    xbc = jax.nn.silu(depthwise_conv_centred(xbc, conv_w, conv_b)).astype(jnp.float32)
    gn = SSD_GROUPS * SSD_STATE
    xs = xbc[..., :BRANCH_W].reshape(b, n, SSD_HEADS, SSD_HEAD_DIM)
    bm = xbc[..., BRANCH_W:BRANCH_W + gn].reshape(b, n, SSD_GROUPS, SSD_STATE)
    cm = xbc[..., BRANCH_W + gn:].reshape(b, n, SSD_GROUPS, SSD_STATE)
    dts = jax.nn.softplus(dt_raw.astype(jnp.float32)[None]
                          + dt_bias.astype(jnp.float32)[:, None, None, :])
    return xs, bm, cm, dts


def ssd_mixer(lat, ctx, conv_w, conv_b, dt_bias, a_log, d, norm_g, ctx_out):
    z_l, xbc_l, dtr_l = lat
    z_c, xbc_c, dtr_c = ctx
    a = -jnp.exp(a_log.astype(jnp.float32))
    xl, bl, cl, dtl = ssd_prepare(xbc_l, dtr_l, conv_w, conv_b, dt_bias)
    xc, bc, cc, dtc = ssd_prepare(xbc_c, dtr_c, conv_w, conv_b, dt_bias)
    b = xl.shape[0]
    zero = jnp.zeros((b, SSD_HEADS, SSD_HEAD_DIM, SSD_STATE), jnp.float32)
    d_h = d.astype(jnp.float32)[:, None]
    y_l = xl * d_h
    y_c = xc * d_h if ctx_out else None
    for direction in range(2):
        flip = direction == 1
        yc_dir, hc = ssd_chunked(maybe_flip(xc, flip), maybe_flip(dtc[direction], flip), a[direction],
                                 maybe_flip(bc, flip), maybe_flip(cc, flip), zero, ctx_out)
        yl_dir, _ = ssd_chunked(maybe_flip(xl, flip), maybe_flip(dtl[direction], flip), a[direction],
                                maybe_flip(bl, flip), maybe_flip(cl, flip), hc, True)
        y_l = y_l + maybe_flip(yl_dir, flip)
        if ctx_out:
            y_c = y_c + maybe_flip(yc_dir, flip)

    def finish(y, z):
        return rms_norm(y.reshape(z.shape).astype(z.dtype) * jax.nn.silu(z), norm_g)

    return finish(y_l, z_l), (finish(y_c, z_c) if ctx_out else None)


def diff_heads(t, rope):
    b, n = t.shape[:2]
    t = t.reshape(b, n, DIFF_HEADS * 2, DIFF_HEAD_DIM)
    if rope is not None:
        t = apply_rope(t, *rope)
    return t.reshape(b, n, DIFF_HEADS, 2, DIFF_HEAD_DIM)


def diff_values(v):
    return v.reshape(v.shape[0], v.shape[1], DIFF_HEADS, 2 * DIFF_HEAD_DIM)


def diff_attend(q, k, v, lam, scale):
    s = jnp.einsum('bqhcd,bkhcd->bhcqk', q, k).astype(jnp.float32) * scale
    p = jax.nn.softmax(s, axis=-1)
    w = (p[:, :, 0] - lam * p[:, :, 1]).astype(v.dtype)
    return jnp.einsum('bhqk,bkhd->bqhd', w, v)


def diff_mixer(lat, ctx, lam_q, lam_k, norm_g, lam_init, rope, ctx_out):
    q_l, k_l, v_l = lat
    q_c, k_c, v_c = ctx
    scale = DIFF_HEAD_DIM ** -0.5
    lq = lam_q.astype(jnp.float32)
    lk = lam_k.astype(jnp.float32)
    lam = jnp.exp(jnp.sum(lq[0] * lk[0])) - jnp.exp(jnp.sum(lq[1] * lk[1])) + lam_init
    kc = diff_heads(k_c, None)
    vc = diff_values(v_c)
    k_all = jnp.concatenate([diff_heads(k_l, rope), kc], axis=1)
    v_all = jnp.concatenate([diff_values(v_l), vc], axis=1)
    y_l = sweep_query_blocks(lambda qb: diff_attend(qb, k_all, v_all, lam, scale),
                             diff_heads(q_l, rope))

    def finish(y):
        return flatten_heads(rms_norm(y, norm_g) * (1.0 - lam_init))

    if not ctx_out:
        return finish(y_l), None
    y_c = diff_attend(diff_heads(q_c, None), kc, vc, lam, scale)
    return finish(y_l), finish(y_c)


def merge_branches(branches, gate_pre, b_merge, w_branch, w_out):
    merged = None
    for n, y in enumerate(branches):
        g = jax.nn.sigmoid(gate_pre[..., n * D_MODEL:(n + 1) * D_MODEL] + b_merge[n])
        term = g * (y @ w_branch[n])
        merged = term if merged is None else merged + term
    return merged @ w_out


def hybrid_layer(xl, xc, c, c_ctx, p, rope, layer_idx, ctx_out):
    mod_l = jax.nn.silu(c) @ p['w_ada'] + p['b_ada']
    mod_c = jax.nn.silu(c_ctx) @ p['w_ada'] + p['b_ada']
    shift_l, scale_l, gate_l = jnp.split(mod_l[:, None, :], 3, axis=-1)
    shift_c, scale_c, gate_c = jnp.split(mod_c, 3, axis=-1)
    hl = layer_norm(xl) * (1.0 + scale_l) + shift_l
    hc = layer_norm(xc) * (1.0 + scale_c) + shift_c
    (cq_l, ckv_l, kr_l, ga_l, u_l, gb_l, z_l, xbc_l, dt_l,
     qd_l, kd_l, vd_l, gd_l, gm_l) = jnp.split(hl @ p['w_in'], SPLIT_POINTS, axis=-1)
    (cq_c, ckv_c, kr_c, ga_c, u_c, gb_c, z_c, xbc_c, dt_c,
     qd_c, kd_c, vd_c, gd_c, gm_c) = jnp.split(hc @ p['w_in'], SPLIT_POINTS, axis=-1)

    ya_l, ya_c = mla_mixer((cq_l, ckv_l, kr_l), (cq_c, ckv_c, kr_c), p['mla_q_norm'],
                           p['mla_w_uq'], p['mla_kv_norm'], p['mla_w_ukv'], rope, ctx_out)
    yb_l, yb_c = s5_mixer(u_l, u_c, p['s5_lambda_re'], p['s5_lambda_im'], p['s5_log_dt'],
                          p['s5_b_re'], p['s5_b_im'], p['s5_c_re'], p['s5_c_im'], p['s5_d'],
                          p['s5_w_glu'], p['s5_b_glu'], ctx_out)
    yc_l, yc_c = ssd_mixer((z_l, xbc_l, dt_l), (z_c, xbc_c, dt_c), p['ssd_conv_w'], p['ssd_conv_b'],
                           p['ssd_dt_bias'], p['ssd_a_log'], p['ssd_d'], p['ssd_norm'], ctx_out)
    lam_init = 0.8 - 0.6 * math.exp(-0.3 * layer_idx)
    yd_l, yd_c = diff_mixer((qd_l, kd_l, vd_l), (qd_c, kd_c, vd_c), p['diff_lambda_q'],
                            p['diff_lambda_k'], p['diff_norm'], lam_init, rope, ctx_out)

    def merge(ya, yb, yc, yd, ga, gb, gd, gm):
        branches = (ya * jax.nn.silu(ga), yb * jax.nn.silu(gb), yc, yd * jax.nn.silu(gd))
        return merge_branches(branches, gm, p['b_merge'], p['w_branch'], p['w_out'])

    out_l = merge(ya_l, yb_l, yc_l, yd_l, ga_l, gb_l, gd_l, gm_l)
    xl_new = layer_norm(DEEPNORM_ALPHA * xl + gate_l * out_l) * p['ln_g'] + p['ln_b']
    if not ctx_out:
        return xl_new, None
    out_c = merge(ya_c, yb_c, yc_c, yd_c, ga_c, gb_c, gd_c, gm_c)
    xc_new = layer_norm(DEEPNORM_ALPHA * xc + gate_c * out_c) * p['ln_g'] + p['ln_b']
    return xl_new, xc_new


def setup_inputs(seed: int = 0) -> dict:
    key = jax.random.key(seed)
    keys = iter(jax.random.split(key, 48))
    f32 = jnp.float32

    def normal(shape, scale):
        return scale * jax.random.normal(next(keys), shape, f32)

    def uniform(shape, lo, hi):
        return jax.random.uniform(next(keys), shape, f32, lo, hi)

    def gain(shape):
        return 1.0 + normal(shape, 0.01)

    L, D = DEPTH, D_MODEL
    G, P, S = S5_GROUPS, S5_STATE, S5_GROUP
    ssd_dt = jnp.exp(uniform((L, 2, SSD_HEADS), math.log(SSD_DT_MIN), math.log(SSD_DT_MAX)))
    return {
        'x': normal((BATCH, SEQ, D), 1.0),
        'c': normal((BATCH, D), 1.0),
        'ctx': normal((BATCH, CTX_LEN, D), 1.0),
        'c_ctx': normal((D,), 1.0),
        'w_ada': normal((L, D, 3 * D), D ** -0.5),
        'b_ada': normal((L, 3 * D), 0.01),
        'w_in': normal((L, D, N_IN), D ** -0.5),
        'mla_q_norm': gain((L, MLA_Q_LORA)),
        'mla_w_uq': normal((L, MLA_Q_LORA, MLA_HEADS * (MLA_NOPE + MLA_ROPE)), MLA_Q_LORA ** -0.5),
        'mla_kv_norm': gain((L, MLA_KV_LORA)),
        'mla_w_ukv': normal((L, MLA_KV_LORA, MLA_HEADS * (MLA_NOPE + MLA_V)), MLA_KV_LORA ** -0.5),
        's5_lambda_re': -0.5 + normal((L, 2, G, P), 0.01),
        's5_lambda_im': math.pi * jnp.arange(P, dtype=f32) + normal((L, 2, G, P), 0.01),
        's5_log_dt': uniform((L, 2, G), math.log(S5_DT_MIN), math.log(S5_DT_MAX)),
        's5_b_re': normal((L, 2, G, P, S), (2 * S) ** -0.5),
        's5_b_im': normal((L, 2, G, P, S), (2 * S) ** -0.5),
        's5_c_re': normal((L, 2, G, S, P), (2 * P) ** -0.5),
        's5_c_im': normal((L, 2, G, S, P), (2 * P) ** -0.5),
        's5_d': normal((L, BRANCH_W), 1.0),
        's5_w_glu': normal((L, BRANCH_W, BRANCH_W), BRANCH_W ** -0.5),
        's5_b_glu': normal((L, BRANCH_W), 0.01),
        'ssd_conv_w': normal((L, SSD_CONV, SSD_CONV_CH), SSD_CONV ** -0.5),
        'ssd_conv_b': normal((L, SSD_CONV_CH), 0.01),
        'ssd_dt_bias': ssd_dt + jnp.log(-jnp.expm1(-ssd_dt)),
        'ssd_a_log': jnp.log(uniform((L, 2, SSD_HEADS), 1.0, 16.0)),
        'ssd_d': gain((L, SSD_HEADS)),
        'ssd_norm': gain((L, BRANCH_W)),
        'diff_lambda_q': normal((L, 2, DIFF_HEAD_DIM), 0.1),
        'diff_lambda_k': normal((L, 2, DIFF_HEAD_DIM), 0.1),
        'diff_norm': gain((L, 2 * DIFF_HEAD_DIM)),
        'b_merge': normal((L, N_BRANCH, D), 0.01),
        'w_branch': normal((L, N_BRANCH, BRANCH_W, D), DEEPNORM_BETA * BRANCH_W ** -0.5),
        'w_out': normal((L, D, D), DEEPNORM_BETA * D ** -0.5),
        'ln_g': gain((L, D)),
        'ln_b': normal((L, D), 0.01),
    }


def reference(x, c, ctx, c_ctx, w_ada, b_ada, w_in, mla_q_norm, mla_w_uq, mla_kv_norm, mla_w_ukv,
              s5_lambda_re, s5_lambda_im, s5_log_dt, s5_b_re, s5_b_im, s5_c_re, s5_c_im, s5_d,
              s5_w_glu, s5_b_glu, ssd_conv_w, ssd_conv_b, ssd_dt_bias, ssd_a_log, ssd_d, ssd_norm,
              diff_lambda_q, diff_lambda_k, diff_norm, b_merge, w_branch, w_out, ln_g, ln_b):
    rows = x.shape[1] // GRID_W
    row_id = jnp.repeat(jnp.arange(rows, dtype=jnp.float32), GRID_W)
    col_id = jnp.tile(jnp.arange(GRID_W, dtype=jnp.float32), rows)
    rope = rope_tables(row_id, col_id, ROPE_DIM)
    xl, xc = x, ctx
    for i in range(DEPTH):
        p = dict(
            w_ada=w_ada[i], b_ada=b_ada[i], w_in=w_in[i],
            mla_q_norm=mla_q_norm[i], mla_w_uq=mla_w_uq[i],
            mla_kv_norm=mla_kv_norm[i], mla_w_ukv=mla_w_ukv[i],
            s5_lambda_re=s5_lambda_re[i], s5_lambda_im=s5_lambda_im[i], s5_log_dt=s5_log_dt[i],
            s5_b_re=s5_b_re[i], s5_b_im=s5_b_im[i], s5_c_re=s5_c_re[i], s5_c_im=s5_c_im[i],
            s5_d=s5_d[i], s5_w_glu=s5_w_glu[i], s5_b_glu=s5_b_glu[i],
            ssd_conv_w=ssd_conv_w[i], ssd_conv_b=ssd_conv_b[i], ssd_dt_bias=ssd_dt_bias[i],
            ssd_a_log=ssd_a_log[i], ssd_d=ssd_d[i], ssd_norm=ssd_norm[i],
            diff_lambda_q=diff_lambda_q[i], diff_lambda_k=diff_lambda_k[i], diff_norm=diff_norm[i],
            b_merge=b_merge[i], w_branch=w_branch[i], w_out=w_out[i], ln_g=ln_g[i], ln_b=ln_b[i],
        )
        xl, xc = hybrid_layer(xl, xc, c, c_ctx, p, rope, i, i < DEPTH - 1)
    return xl
```

```python
from concourse.bass_utils import run_bass_kernel_spmd
import math
import numpy as np
import concourse.bass as bass
import concourse.mybir as mybir

F32 = mybir.dt.float32
BF16 = mybir.dt.bfloat16
I32 = mybir.dt.int32
AF = mybir.ActivationFunctionType
ALU = mybir.AluOpType
AX = mybir.AxisListType

EPOCH = 30000


class Sem:
    def __init__(self, nc, name):
        self.h = nc.semaphore(name).__enter__()
        self.v = 0


class T:
    def __init__(self, t, name=""):
        self.t = t
        self.name = name
        self.w = {}
        self.r = {}

    def __getitem__(self, idx):
        return self.t[idx]

    def ap(self):
        return self.t.ap() if hasattr(self.t, "ap") and callable(self.t.ap) else self.t


class Q:
    def __init__(self, fw, name, eng, is_pe=False):
        self.fw = fw
        self.name = name
        self.eng = eng
        self.is_pe = is_pe
        self.nsem = 0
        self.sem = None
        self.waited = {}
        self.new_sem()

    def new_sem(self):
        self.sem = Sem(self.fw.nc, f"q_{self.name}_{self.nsem}")
        self.nsem += 1


class FW:
    def __init__(self, nc, n_dma_sems=24):
        self.nc = nc
        self.pe = Q(self, "pe", nc.tensor, True)
        self.dve = Q(self, "dve", nc.vector)
        self.act = Q(self, "act", nc.scalar)
        self.pool = Q(self, "pool", nc.gpsimd)
        self.sp = Q(self, "sp", nc.sync)
        self.queues = [self.pe, self.dve, self.act, self.pool, self.sp]
        self.dma_sems = [Sem(nc, f"dma{i}") for i in range(n_dma_sems)]
        self.dma_i = 0
        self.ninst = 0
        self.stack = []

    def sb(self, name, shape, dtype=F32):
        self.uid = getattr(self, "uid", 0) + 1
        name = f"{name}_u{self.uid}"
        cm = self.nc.sbuf_tensor(name, list(shape), dtype)
        t = cm.__enter__()
        self.stack.append(cm)
        return T(t, name)

    def ps(self, name, shape, dtype=F32):
        self.uid = getattr(self, "uid", 0) + 1
        name = f"{name}_u{self.uid}"
        cm = self.nc.psum_tensor(name, list(shape), dtype)
        t = cm.__enter__()
        self.stack.append(cm)
        return T(t, name)

    def mark(self):
        return len(self.stack)

    def release(self, mark):
        self.barrier()
        while len(self.stack) > mark:
            cm = self.stack.pop()
            cm.__exit__(None, None, None)

    def dram(self, name, shape, dtype, kind="Internal"):
        t = self.nc.dram_tensor(name, list(shape), dtype, kind=kind)
        return T(t.ap(), name)

    def _deps(self, q, reads, writes):
        deps = {}
        for t in reads:
            for s, v in t.w.items():
                if deps.get(s, 0) < v:
                    deps[s] = v
        for t in writes:
            for s, v in t.w.items():
                if deps.get(s, 0) < v:
                    deps[s] = v
            for s, v in t.r.items():
                if deps.get(s, 0) < v:
                    deps[s] = v
        for s, v in deps.items():
            if q.is_pe and s is q.sem:
                continue
            if q.waited.get(s, 0) < v:
                q.eng.wait_ge(s.h, v)
                q.waited[s] = v

    def _post(self, sem, val, reads, writes):
        for t in reads:
            if t.r.get(sem, 0) < val:
                t.r[sem] = val
        for t in writes:
            t.w = {sem: val}
            t.r = {}

    def op(self, q, fn, reads=(), writes=()):
        self._deps(q, reads, writes)
        if q.sem.v >= EPOCH:
            q.new_sem()
        inst = fn()
        q.sem.v += 1
        inst.then_inc(q.sem.h, 1)
        self._post(q.sem, q.sem.v, reads, writes)
        self.ninst += 1
        return inst

    def dma(self, out, in_, reads=(), writes=(), q=None, **kw):
        q = q or self.sp
        self._deps(q, reads, writes)
        s = self.dma_sems[self.dma_i]
        self.dma_i = (self.dma_i + 1) % len(self.dma_sems)
        if q.waited.get(s, 0) < s.v:
            q.eng.wait_ge(s.h, s.v)
            q.waited[s] = s.v
        if s.v >= EPOCH:
            s2 = Sem(self.nc, f"dma_r{self.ninst}")
            self.dma_sems[(self.dma_i - 1) % len(self.dma_sems)] = s2
            s = s2
        inst = q.eng.dma_start(out=out, in_=in_, **kw)
        s.v += 16
        inst.then_inc(s.h, 16)
        self._post(s, s.v, reads, writes)
        self.ninst += 1
        return inst

    def barrier(self):
        allv = {}
        for q in self.queues:
            if q.sem.v > 0:
                allv[q.sem] = q.sem.v
        for s in self.dma_sems:
            if s.v > 0:
                allv[s] = s.v
        for q in self.queues:
            for s, v in allv.items():
                if q.waited.get(s, 0) < v:
                    q.eng.wait_ge(s.h, v)
                    q.waited[s] = v

    def mm(self, out_ap, lhsT_ap, rhs_ap, start, stop, reads, writes, **kw):
        return self.op(self.pe, lambda: self.nc.tensor.matmul(out_ap, lhsT_ap, rhs_ap, start=start, stop=stop, **kw), reads, writes)

    def tr(self, out_ap, in_ap, ident_ap, reads, writes):
        return self.op(self.pe, lambda: self.nc.tensor.transpose(out_ap, in_ap, ident_ap), reads, writes)

    def actf(self, out_ap, in_ap, func, reads, writes, q=None, **kw):
        return self.op(self.act, lambda: self.nc.scalar.activation(out=out_ap, in_=in_ap, func=func, **kw), reads, writes)

    def ts(self, out_ap, in_ap, s1, s2, op0, op1=None, reads=(), writes=(), q=None, **kw):
        q = q or self.dve
        if op1 is None:
            return self.op(q, lambda: q.eng.tensor_scalar(out=out_ap, in0=in_ap, scalar1=s1, scalar2=None, op0=op0, **kw), reads, writes)
        return self.op(q, lambda: q.eng.tensor_scalar(out=out_ap, in0=in_ap, scalar1=s1, scalar2=s2, op0=op0, op1=op1, **kw), reads, writes)

    def tt(self, out_ap, a_ap, b_ap, op, reads=(), writes=(), q=None):
        q = q or self.dve
        return self.op(q, lambda: q.eng.tensor_tensor(out=out_ap, in0=a_ap, in1=b_ap, op=op), reads, writes)

    def stt(self, out_ap, in0, scalar, in1, op0, op1, reads=(), writes=(), q=None):
        q = q or self.dve
        return self.op(q, lambda: q.eng.scalar_tensor_tensor(out=out_ap, in0=in0, scalar=scalar, in1=in1, op0=op0, op1=op1), reads, writes)

    def cp(self, out_ap, in_ap, reads=(), writes=(), q=None):
        q = q or self.dve
        return self.op(q, lambda: q.eng.tensor_copy(out=out_ap, in_=in_ap), reads, writes)

    def memset(self, ap, val, writes=(), q=None):
        q = q or self.dve
        return self.op(q, lambda: q.eng.memset(ap, val), (), writes)


T_ALL = 4352
NT = 34
D = 2048
KT = 16
N_IN = 19280
BLOCKS = [(0, 256)] + [(256 + 512 * i, 512) for i in range(8)]
LN_EPS = 1e-5
RMS_EPS = 1e-6

SEC = {}
_off = 0
for _n, _w in [("cq", 512), ("ckv", 256), ("kr", 64), ("ga", 1024), ("u", 1024), ("gb", 1024), ("z", 1024),
               ("xbc", 2048), ("dt", 16), ("qd", 1024), ("kd", 1024), ("vd", 1024), ("gd", 1024), ("gm", 8192)]:
    SEC[_n] = (_off, _w)
    _off += _w
assert _off == N_IN


def load_featmajor(f, dst, src_ap_2d, nrows, ident_f, ps_t):
    tmp = f.sb(f"lfm_{dst.name}", [128, 128], F32)
    f.memset(tmp[:], 0.0, writes=[tmp])
    f.dma(tmp[0:nrows, :], src_ap_2d, reads=[], writes=[tmp])
    f.tr(ps_t[:, 0:128], tmp[:, :], ident_f[:], reads=[tmp, ident_f], writes=[ps_t])
    f.cp(dst[:, 0:nrows], ps_t[:, 0:nrows], reads=[ps_t], writes=[dst])


def make_ident(f, name="ident"):
    nc = f.nc
    identf = f.sb(name + "_f", [128, 128], F32)
    ident = f.sb(name + "_b", [128, 128], BF16)
    f.memset(identf[:], 0.0, writes=[identf], q=f.pool)
    f.op(f.pool, lambda: nc.gpsimd.affine_select(out=identf[:], in_=identf[:], pattern=[[-1, 128]], compare_op=ALU.not_equal, fill=1.0, base=0, channel_multiplier=1), reads=[identf], writes=[identf])
    f.cp(ident[:], identf[:], reads=[identf], writes=[ident])
    return identf, ident


def phase_mods(f, env, l):
    nc = f.nc
    m0 = f.mark()
    mods = {n: f.sb("mod_" + n, [128, D], F32) for n in ["shiftL", "scaleL", "gateL", "shiftC", "scaleC", "gateC"]}
    ones = f.sb("m_ones", [128, 128], F32)
    f.memset(ones[:], 1.0, writes=[ones])
    cs = f.sb("m_cs", [128, 2, KT], F32)
    crow = f.sb("m_crow", [32, 128], F32)
    ps = f.ps("m_ps", [128, 512], F32)
    ps2 = f.ps("m_ps2", [128, 512], F32)
    identf = env["identf"]
    f.memset(crow[:], 0.0, writes=[crow])
    f.dma(crow[0:16, :], env["c"].t[0, :].rearrange("(k p) -> k p", p=128), writes=[crow])
    f.dma(crow[16:32, :], env["c_ctx"].t[0, :].rearrange("(k p) -> k p", p=128), writes=[crow])
    f.actf(crow[:], crow[:], AF.Silu, reads=[crow], writes=[crow])
    f.tr(ps[:, 0:32], crow[:, :], identf[0:32, 0:32], reads=[crow, identf], writes=[ps])
    f.cp(cs[:].rearrange("p a k -> p (a k)"), ps[:, 0:32], reads=[ps], writes=[cs])
    lh = [f.sb(f"m_lh{a}", [128, KT, 128], F32) for a in range(2)]
    for a in range(2):
        for k in range(KT):
            f.ts(lh[a][:, k, :], ones[:], cs[:, a, k:k + 1], None, ALU.mult, reads=[ones, cs], writes=[lh[a]])
    wbuf = [f.sb(f"m_w{i}", [128, KT, 512], F32) for i in range(2)]
    brow = [f.sb(f"m_b{i}", [128, 512], F32) for i in range(2)]
    names = [("shiftL", "shiftC"), ("scaleL", "scaleC"), ("gateL", "gateC")]
    for ci in range(12):
        wb = wbuf[ci % 2]
        bb = brow[ci % 2]
        c0 = ci * 512
        f.dma(wb[:], env["w_ada"].t[l, :, c0:c0 + 512].rearrange("(k p) c -> p k c", p=128), writes=[wb])
        f.dma(bb[:], env["b_ada"].t[l:l + 1, c0:c0 + 512].partition_broadcast(128), writes=[bb])
        which = c0 // 2048
        cc = c0 % 2048
        for a in range(2):
            p = ps if a == 0 else ps2
            for k in range(KT):
                f.mm(p[:, :], lh[a][:, k, :], wb[:, k, :], k == 0, k == KT - 1, reads=[lh[a], wb], writes=[p])
            dst = mods[names[which][a]]
            f.tt(dst[:, cc:cc + 512], p[:, :], bb[:], ALU.add, reads=[p, bb], writes=[dst])
            if which == 1:
                f.ts(dst[:, cc:cc + 512], dst[:, cc:cc + 512], 1.0, None, ALU.add, reads=[dst], writes=[dst])
    for i, n in enumerate(["shiftL", "scaleL", "gateL", "shiftC", "scaleC", "gateC"]):
        f.dma(env["modrows"].t[i:i + 1, :], mods[n][0:1, :], reads=[mods[n]], writes=[env["modrows"]])
    f.release(m0)


def phase_ln(f, env, l, xsrc):
    nc = f.nc
    m0 = f.mark()
    mods = {}
    for i, n in enumerate(["shiftL", "scaleL", "gateL", "shiftC", "scaleC", "gateC"]):
        if n.startswith("gate"):
            continue
        mods[n] = f.sb("lnm_" + n, [128, D], F32)
        f.dma(mods[n][:], env["modrows"].t[i:i + 1, :].partition_broadcast(128), reads=[env["modrows"]], writes=[mods[n]])
    ident = env["ident"]
    hT = env["hT"]
    xt = [f.sb(f"ln_x{i}", [128, D], F32) for i in range(2)]
    hb = [f.sb(f"ln_h{i}", [128, D], BF16) for i in range(2)]
    ht = [f.sb(f"ln_t{i}", [128, KT, 128], BF16) for i in range(2)]
    st = [f.sb(f"ln_s{i}", [128, 4, 6], F32) for i in range(2)]
    mv = [f.sb(f"ln_mv{i}", [128, 2], F32) for i in range(2)]
    rs = [f.sb(f"ln_rs{i}", [128, 1], F32) for i in range(2)]
    pst = [f.ps(f"ln_ps{i}", [128, 2048], BF16) for i in range(2)]
    epst = f.sb("ln_eps", [128, 1], F32)
    f.memset(epst[:], LN_EPS, writes=[epst])
    for i in range(NT):
        b = i % 2
        t0 = i * 128
        ctx = i < 2
        f.dma(xt[b][:], xsrc.t[t0:t0 + 128, :], reads=[xsrc], writes=[xt[b]])
        for c in range(4):
            f.op(f.dve, lambda c=c: nc.vector.bn_stats(out=st[b][:, c, :], in_=xt[b][:, c * 512:(c + 1) * 512]), reads=[xt[b]], writes=[st[b]])
        f.op(f.dve, lambda: nc.vector.bn_aggr(out=mv[b][:], in_=st[b][:].rearrange("p a s -> p (a s)")), reads=[st[b]], writes=[mv[b]])
        f.actf(rs[b][:], mv[b][:, 1:2], AF.Sqrt, reads=[mv[b], epst], writes=[rs[b]], bias=epst[:, 0:1], scale=1.0)
        f.op(f.dve, lambda: nc.vector.reciprocal(out=rs[b][:], in_=rs[b][:]), reads=[rs[b]], writes=[rs[b]])
        f.ts(xt[b][:], xt[b][:], mv[b][:, 0:1], rs[b][:, 0:1], ALU.subtract, ALU.mult, reads=[xt[b], mv[b], rs[b]], writes=[xt[b]])
        sc = mods["scaleC" if ctx else "scaleL"]
        sh = mods["shiftC" if ctx else "shiftL"]
        f.tt(xt[b][:], xt[b][:], sc[:], ALU.mult, reads=[xt[b], sc], writes=[xt[b]], q=f.pool)
        f.tt(hb[b][:], xt[b][:], sh[:], ALU.add, reads=[xt[b], sh], writes=[hb[b]], q=f.pool)
        for k in range(KT):
            f.tr(pst[b][:, k * 128:(k + 1) * 128], hb[b][:, k * 128:(k + 1) * 128], ident[:], reads=[hb[b], ident], writes=[pst[b]])
        f.cp(ht[b][:].rearrange("p k t -> p (k t)"), pst[b][:], reads=[pst[b]], writes=[ht[b]])
        f.dma(hT.t[:, :, t0:t0 + 128].rearrange("k p t -> p k t"), ht[b][:], reads=[ht[b]], writes=[hT])
    f.release(m0)


def phase_inproj(f, env, l, sections=None):
    nc = f.nc
    m0 = f.mark()
    hT = env["hT"]
    w_in = env["w_in"]
    identf = env["identf"]
    pss = [f.ps(f"ip_ps{i}", [128, 512], F32) for i in range(4)]
    pst = f.ps("ip_pst", [128, 512], F32)
    bm = f.sb("ip_bm", [128, 64], F32)
    load_featmajor(f, bm, env["b_merge"].t[l].rearrange("n (k p) -> (n k) p", p=128), 64, identf, pst)
    CW = 1024
    wbuf = [f.sb(f"ip_w{i}", [128, KT, CW], BF16) for i in range(2)]
    hbuf = [f.sb(f"ip_h{i}", [128, KT, 512], BF16) for i in range(2)]
    ev32 = [f.sb(f"ip_e32_{i}", [128, 512], F32) for i in range(3)]
    ev16 = [f.sb(f"ip_e16_{i}", [128, 512], BF16) for i in range(3)]
    fm = {
        "cq": ("cqT", AF.Identity, F32), "ckv": ("ckvT", AF.Identity, F32), "kr": ("krT", AF.Identity, F32),
        "ga": ("gaT", AF.Silu, BF16), "u": ("uT", AF.Identity, F32), "gb": ("gbT", AF.Silu, BF16),
        "z": ("zT", AF.Silu, BF16), "xbc": ("xbcT", AF.Identity, F32), "qd": ("qdT", AF.Identity, F32),
        "kd": ("kdT", AF.Identity, F32), "gd": ("gdT", AF.Silu, BF16), "gm": ("gmT", AF.Sigmoid, BF16),
    }
    slabs = []
    for name, (o, w) in SEC.items():
        if sections is not None and name not in sections:
            continue
        c = 0
        while c < w:
            cw = min(CW, w - c)
            slabs.append((o + c, cw, name, c))
            c += cw
    si = 0
    hi = 0
    pi = 0
    ei = 0
    for (c0, cw, name, so) in slabs:
        wb = wbuf[si % 2]
        si += 1
        f.dma(wb[:, :, 0:cw], w_in.t[l, :, c0:c0 + cw].rearrange("(k p) c -> p k c", p=128), reads=[], writes=[wb], q=f.pool)
        for (t0, nb) in BLOCKS:
            hb = hbuf[hi % 2]
            hi += 1
            f.dma(hb[:, :, 0:nb], hT.t[:, :, t0:t0 + nb].rearrange("k p t -> p k t"), reads=[hT], writes=[hb])
            if name in fm:
                dname, func, dt_ = fm[name]
                dst = env[dname]
                for ct in range((cw + 127) // 128):
                    m = min(128, cw - ct * 128)
                    p = pss[pi % 4]
                    pi += 1
                    for k in range(KT):
                        f.mm(p[0:m, 0:nb], wb[:, k, ct * 128:ct * 128 + m], hb[:, k, 0:nb], k == 0, k == KT - 1, reads=[wb, hb], writes=[p])
                    ev = (ev32 if dt_ == F32 else ev16)[ei % 3]
                    ei += 1
                    r0 = so + ct * 128
                    if name == "gm":
                        col = r0 // 128
                        f.actf(ev[0:m, 0:nb], p[0:m, 0:nb], func, reads=[p, bm], writes=[ev], bias=bm[:, col:col + 1], scale=1.0)
                    elif func == AF.Identity:
                        f.cp(ev[0:m, 0:nb], p[0:m, 0:nb], reads=[p], writes=[ev])
                    else:
                        f.actf(ev[0:m, 0:nb], p[0:m, 0:nb], func, reads=[p], writes=[ev])
                    f.dma(dst.t[r0:r0 + m, t0:t0 + nb], ev[0:m, 0:nb], reads=[ev], writes=[dst])
            else:
                for tt_ in range(nb // 128):
                    for cc in range(0, cw, 512):
                        n = min(512, cw - cc)
                        p = pss[pi % 4]
                        pi += 1
                        for k in range(KT):
                            f.mm(p[:, 0:n], hb[:, k, tt_ * 128:(tt_ + 1) * 128], wb[:, k, cc:cc + n], k == 0, k == KT - 1, reads=[wb, hb], writes=[p])
                        if name == "vd":
                            ev = ev16[ei % 3]
                            ei += 1
                            f.cp(ev[:, 0:n], p[:, 0:n], reads=[p], writes=[ev])
                            f.dma(env["Vd"].t[t0 + tt_ * 128:t0 + (tt_ + 1) * 128, so + cc:so + cc + n], ev[:, 0:n], reads=[ev], writes=[env["Vd"]])
                        else:
                            ev = ev32[ei % 3]
                            ei += 1
                            f.cp(ev[:, 0:n], p[:, 0:n], reads=[p], writes=[ev])
                            f.dma(env["dtraw"].t[t0 + tt_ * 128:t0 + (tt_ + 1) * 128, 0:n], ev[:, 0:n], reads=[ev], writes=[env["dtraw"]])
    f.release(m0)


def consts(f, env):
    env["ones_f"] = f.sb("c_ones_f", [128, 128], F32)
    env["ones_b"] = f.sb("c_ones_b", [128, 128], BF16)
    f.memset(env["ones_f"][:], 1.0, writes=[env["ones_f"]])
    f.memset(env["ones_b"][:], 1.0, writes=[env["ones_b"]])
    env["eps_rms"] = f.sb("c_eps_rms", [128, 1], F32)
    f.memset(env["eps_rms"][:], RMS_EPS, writes=[env["eps_rms"]])


def rms_rstd(f, env, src, K, nfeat, nb, sq, ps, rstd):
    nc = f.nc
    f.tt(sq[:, 0:K, 0:nb], src[:, 0:K, 0:nb], src[:, 0:K, 0:nb], ALU.mult, reads=[src], writes=[sq])
    for k in range(K):
        f.mm(ps[:, 0:nb], env["ones_f"][:], sq[:, k, 0:nb], k == 0, k == K - 1, reads=[env["ones_f"], sq], writes=[ps])
    f.actf(rstd[:, 0:nb], ps[:, 0:nb], AF.Sqrt, reads=[ps, env["eps_rms"]], writes=[rstd], bias=env["eps_rms"][:, 0:1], scale=1.0 / nfeat)
    f.op(f.dve, lambda: nc.vector.reciprocal(out=rstd[:, 0:nb], in_=rstd[:, 0:nb]), reads=[rstd], writes=[rstd])


def phase_mla_prep(f, env, l):
    nc = f.nc
    m0 = f.mark()
    identf = env["identf"]
    C2 = f.sb("rp_C2", [128, 4096], F32)
    S2 = f.sb("rp_S2", [128, 4096], F32)
    f.dma(C2[:], env["ropeC"].t[:, :], writes=[C2])
    f.dma(S2[:], env["ropeS"].t[:, :], writes=[S2])
    pst = f.ps("mp_pst", [128, 512], F32)
    ps_s = f.ps("mp_pss", [128, 512], F32)
    pss = [f.ps(f"mp_ps{i}", [128, 512], F32) for i in range(4)]
    wraw = f.sb("mp_wraw", [128, 4, 1536], F32)
    f.dma(wraw[:], env["mla_w_uq"].t[l].rearrange("(k p) c -> p k c", p=128), writes=[wraw])
    wn = f.sb("mp_wn", [128, 4, 8, 128], BF16)
    wA = f.sb("mp_wA", [128, 4, 8, 64], BF16)
    wB = f.sb("mp_wB", [128, 4, 8, 64], BF16)
    for k in range(4):
        src = wraw[:, k, :].rearrange("p (h c) -> p h c", c=192)
        f.cp(wn[:, k, :, :], src[:, :, 0:128], reads=[wraw], writes=[wn])
        for h in range(8):
            rp = wraw[:, k, h * 192 + 128:h * 192 + 192].rearrange("p (j e) -> p e j", e=2)
            f.cp(wA[:, k, h, 0:32], rp[:, 0, :], reads=[wraw], writes=[wA], q=f.pool)
            f.cp(wA[:, k, h, 32:64], rp[:, 1, :], reads=[wraw], writes=[wA], q=f.pool)
            f.cp(wB[:, k, h, 0:32], rp[:, 1, :], reads=[wraw], writes=[wB], q=f.pool)
            f.cp(wB[:, k, h, 32:64], rp[:, 0, :], reads=[wraw], writes=[wB], q=f.pool)
    wkraw = f.sb("mp_wkraw", [128, 2, 2048], F32)
    f.dma(wkraw[:], env["mla_w_ukv"].t[l].rearrange("(k p) c -> p k c", p=128), writes=[wkraw])
    wkn = f.sb("mp_wkn", [128, 2, 8, 128], BF16)
    wv = f.sb("mp_wv", [128, 2, 8, 128], BF16)
    for k in range(2):
        src = wkraw[:, k, :].rearrange("p (h c) -> p h c", c=256)
        f.cp(wkn[:, k, :, :], src[:, :, 0:128], reads=[wkraw], writes=[wkn])
        f.cp(wv[:, k, :, :], src[:, :, 128:256], reads=[wkraw], writes=[wv])
    qn = f.sb("mp_qn", [128, 4], F32)
    kvn = f.sb("mp_kvn", [128, 2], F32)
    load_featmajor(f, qn, env["mla_q_norm"].t[l].rearrange("(k p) -> k p", p=128), 4, identf, pst)
    load_featmajor(f, kvn, env["mla_kv_norm"].t[l].rearrange("(k p) -> k p", p=128), 2, identf, pst)
    cq = [f.sb(f"mp_cq{i}", [128, 4, 512], F32) for i in range(2)]
    ckv = [f.sb(f"mp_ckv{i}", [128, 2, 512], F32) for i in range(2)]
    sq = f.sb("mp_sq", [128, 4, 512], F32)
    rstd = f.sb("mp_rstd", [128, 512], F32)
    cqn = f.sb("mp_cqn", [128, 4, 512], BF16)
    ckvn = f.sb("mp_ckvn", [128, 2, 512], BF16)
    st16 = [f.sb(f"mp_st{i}", [128, 512], BF16) for i in range(4)]
    t32 = [f.sb(f"mp_t32_{i}", [128, 512], F32) for i in range(2)]
    krA = f.sb("mp_krA", [64, 512], F32)
    krB = f.sb("mp_krB", [64, 512], F32)
    si = 0
    pi = 0
    for bi, (t0, nb) in enumerate(BLOCKS):
        lat = t0 >= 256
        tl = t0 - 256
        cqb = cq[bi % 2]
        ckb = ckv[bi % 2]
        f.dma(cqb[:, :, 0:nb], env["cqT"].t[:, t0:t0 + nb].rearrange("(k p) t -> p k t", p=128), reads=[env["cqT"]], writes=[cqb])
        f.dma(ckb[:, :, 0:nb], env["ckvT"].t[:, t0:t0 + nb].rearrange("(k p) t -> p k t", p=128), reads=[env["ckvT"]], writes=[ckb])
        rms_rstd(f, env, cqb, 4, 512, nb, sq, ps_s, rstd)
        for k in range(4):
            f.stt(cqn[:, k, 0:nb], cqb[:, k, 0:nb], qn[:, k:k + 1], rstd[:, 0:nb], ALU.mult, ALU.mult, reads=[cqb, qn, rstd], writes=[cqn])
        for h in range(8):
            p = pss[pi % 4]; pi += 1
            for k in range(4):
                f.mm(p[:, 0:nb], wn[:, k, h, :], cqn[:, k, 0:nb], k == 0, k == 3, reads=[wn, cqn], writes=[p])
            st = st16[si % 4]; si += 1
            f.actf(st[:, 0:nb], p[:, 0:nb], AF.Copy, reads=[p], writes=[st])
            f.dma(env["QaT"].t[h, 0:128, t0:t0 + nb], st[:, 0:nb], reads=[st], writes=[env["QaT"]])
            pA = pss[pi % 4]; pi += 1
            for k in range(4):
                f.mm(pA[0:64, 0:nb], wA[:, k, h, :], cqn[:, k, 0:nb], k == 0, k == 3, reads=[wA, cqn], writes=[pA])
            st = st16[si % 4]; si += 1
            if lat:
                pB = pss[pi % 4]; pi += 1
                for k in range(4):
                    f.mm(pB[0:64, 0:nb], wB[:, k, h, :], cqn[:, k, 0:nb], k == 0, k == 3, reads=[wB, cqn], writes=[pB])
                f.tt(t32[0][0:64, 0:nb], pA[0:64, 0:nb], C2[0:64, tl:tl + nb], ALU.mult, reads=[pA, C2], writes=[t32[0]])
                f.tt(t32[1][0:64, 0:nb], pB[0:64, 0:nb], S2[0:64, tl:tl + nb], ALU.mult, reads=[pB, S2], writes=[t32[1]])
                f.tt(st[0:64, 0:nb], t32[0][0:64, 0:nb], t32[1][0:64, 0:nb], ALU.add, reads=[t32[0], t32[1]], writes=[st], q=f.pool)
            else:
                f.actf(st[0:64, 0:nb], pA[0:64, 0:nb], AF.Copy, reads=[pA], writes=[st])
            f.dma(env["QaT"].t[h, 128:192, t0:t0 + nb], st[0:64, 0:nb], reads=[st], writes=[env["QaT"]])
        rms_rstd(f, env, ckb, 2, 256, nb, sq, ps_s, rstd)
        for k in range(2):
            f.stt(ckvn[:, k, 0:nb], ckb[:, k, 0:nb], kvn[:, k:k + 1], rstd[:, 0:nb], ALU.mult, ALU.mult, reads=[ckb, kvn, rstd], writes=[ckvn])
        for h in range(8):
            p = pss[pi % 4]; pi += 1
            for k in range(2):
                f.mm(p[:, 0:nb], wkn[:, k, h, :], ckvn[:, k, 0:nb], k == 0, k == 1, reads=[wkn, ckvn], writes=[p])
            st = st16[si % 4]; si += 1
            f.actf(st[:, 0:nb], p[:, 0:nb], AF.Copy, reads=[p], writes=[st])
            f.dma(env["KaT"].t[h, :, t0:t0 + nb], st[:, 0:nb], reads=[st], writes=[env["KaT"]])
        for tt_ in range(nb // 128):
            for cc in range(2):
                p = pss[pi % 4]; pi += 1
                for k in range(2):
                    f.mm(p[:, :], ckvn[:, k, tt_ * 128:(tt_ + 1) * 128], wv[:, k, cc * 4:(cc + 1) * 4, :].rearrange("p h d -> p (h d)"), k == 0, k == 1, reads=[wv, ckvn], writes=[p])
                st = st16[si % 4]; si += 1
                f.cp(st[:, :], p[:, :], reads=[p], writes=[st])
                f.dma(env["Va"].t[t0 + tt_ * 128:t0 + (tt_ + 1) * 128, cc * 512:(cc + 1) * 512], st[:, :], reads=[st], writes=[env["Va"]])
        krv = env["krT"].t[:, t0:t0 + nb].rearrange("(j e) t -> j e t", e=2)
        f.dma(krA[0:32, 0:nb], krv[:, 0, :], reads=[env["krT"]], writes=[krA])
        f.dma(krA[32:64, 0:nb], krv[:, 1, :], reads=[env["krT"]], writes=[krA])
        st = st16[si % 4]; si += 1
        if lat:
            f.dma(krB[0:32, 0:nb], krv[:, 1, :], reads=[env["krT"]], writes=[krB])
            f.dma(krB[32:64, 0:nb], krv[:, 0, :], reads=[env["krT"]], writes=[krB])
            f.tt(t32[0][0:64, 0:nb], krA[0:64, 0:nb], C2[0:64, tl:tl + nb], ALU.mult, reads=[krA, C2], writes=[t32[0]])
            f.tt(t32[1][0:64, 0:nb], krB[0:64, 0:nb], S2[0:64, tl:tl + nb], ALU.mult, reads=[krB, S2], writes=[t32[1]])
            f.tt(st[0:64, 0:nb], t32[0][0:64, 0:nb], t32[1][0:64, 0:nb], ALU.add, reads=[t32[0], t32[1]], writes=[st], q=f.pool)
        else:
            f.cp(st[0:64, 0:nb], krA[0:64, 0:nb], reads=[krA], writes=[st])
        f.dma(env["KrT"].t[:, t0:t0 + nb], st[0:64, 0:nb], reads=[st], writes=[env["KrT"]])
    f.release(m0)


def phase_diff_prep(f, env, l):
    nc = f.nc
    m0 = f.mark()
    C2 = f.sb("rp_C2", [128, 4096], F32)
    S2 = f.sb("rp_S2", [128, 4096], F32)
    f.dma(C2[:], env["ropeC"].t[:, :], writes=[C2])
    f.dma(S2[:], env["ropeS"].t[:, :], writes=[S2])
    A = [f.sb(f"dp_A{i}", [128, 512], F32) for i in range(2)]
    B = [f.sb(f"dp_B{i}", [128, 512], F32) for i in range(2)]
    t32 = [f.sb(f"dp_t{i}", [128, 512], F32) for i in range(2)]
    st16 = [f.sb(f"dp_st{i}", [128, 512], BF16) for i in range(2)]
    it = 0
    for (srcn, dstn) in (("qdT", "QdT"), ("kdT", "KdT")):
        src, dst = env[srcn], env[dstn]
        for pr in range(8):
            r0 = pr * 128
            for (t0, nb) in BLOCKS:
                lat = t0 >= 256
                tl = t0 - 256
                a = A[it % 2]; b = B[it % 2]; st = st16[it % 2]; it += 1
                for mh in range(2):
                    v = src.t[r0 + mh * 64:r0 + mh * 64 + 64, t0:t0 + nb].rearrange("(j e) t -> j e t", e=2)
                    f.dma(a[mh * 64:mh * 64 + 32, 0:nb], v[:, 0, :], reads=[src], writes=[a])
                    f.dma(a[mh * 64 + 32:mh * 64 + 64, 0:nb], v[:, 1, :], reads=[src], writes=[a])
                    if lat:
                        f.dma(b[mh * 64:mh * 64 + 32, 0:nb], v[:, 1, :], reads=[src], writes=[b])
                        f.dma(b[mh * 64 + 32:mh * 64 + 64, 0:nb], v[:, 0, :], reads=[src], writes=[b])
                if lat:
                    f.tt(t32[0][:, 0:nb], a[:, 0:nb], C2[:, tl:tl + nb], ALU.mult, reads=[a, C2], writes=[t32[0]])
                    f.tt(t32[1][:, 0:nb], b[:, 0:nb], S2[:, tl:tl + nb], ALU.mult, reads=[b, S2], writes=[t32[1]], q=f.pool)
                    f.tt(st[:, 0:nb], t32[0][:, 0:nb], t32[1][:, 0:nb], ALU.add, reads=[t32[0], t32[1]], writes=[st])
                else:
                    f.cp(st[:, 0:nb], a[:, 0:nb], reads=[a], writes=[st])
                f.dma(dst.t[r0:r0 + 128, t0:t0 + nb], st[:, 0:nb], reads=[st], writes=[dst])
    f.release(m0)


def phase_attention(f, env, l, kind):
    nc = f.nc
    m0 = f.mark()
    ones_b, ones_f = env["ones_b"], env["ones_f"]
    identf = env["identf"]
    nmaps = 1 if kind == "mla" else 2
    psS = [f.ps(f"at_S{i}", [128, 512], F32) for i in range(2)]
    psO = [f.ps(f"at_O{i}", [128, 512], F32) for i in range(nmaps)]
    psM = [f.ps(f"at_M{i}", [128, 512], F32) for i in range(nmaps)]
    psF = f.ps("at_F", [128, 512], F32)
    Vt = [f.sb(f"at_V{i}", [128, NT, 128], BF16) for i in range(2)]
    if kind == "mla":
        Kp = [[f.sb(f"at_K0_{i}", [128, T_ALL], BF16), f.sb(f"at_K1_{i}", [64, T_ALL], BF16)] for i in range(2)]
        Qp = [[f.sb(f"at_Q0_{i}", [128, 512], BF16), f.sb(f"at_Q1_{i}", [64, 512], BF16)] for i in range(2)]
        scale = 192 ** -0.5
    else:
        Kp = [[f.sb(f"at_K{m}_{i}", [64, T_ALL], BF16) for m in range(2)] for i in range(2)]
        Qp = [[f.sb(f"at_Q{m}_{i}", [64, 512], BF16) for m in range(2)] for i in range(2)]
        scale = 64 ** -0.5
    Pt = [f.sb(f"at_P{i}", [128, 512], BF16) for i in range(3)]
    rc = [f.sb(f"at_rc{i}", [128, 512], F32) for i in range(2)]
    o = [f.sb(f"at_o{i}", [128, 512], F32) for i in range(2)]
    gate = [f.sb(f"at_g{i}", [128, 512], BF16) for i in range(2)]
    yst = [f.sb(f"at_y{i}", [128, 512], BF16) for i in range(2)]
    if kind == "diff":
        lam_init = 0.8 - 0.6 * float(np.exp(-0.3 * l))
        lq = f.sb("at_lq", [128, 2, 64], F32)
        lk = f.sb("at_lk", [128, 2, 64], F32)
        f.dma(lq[:].rearrange("p a d -> p (a d)"), env["diff_lambda_q"].t[l:l + 1].rearrange("o a d -> o (a d)").partition_broadcast(128), writes=[lq])
        f.dma(lk[:].rearrange("p a d -> p (a d)"), env["diff_lambda_k"].t[l:l + 1].rearrange("o a d -> o (a d)").partition_broadcast(128), writes=[lk])
        f.tt(lq[:], lq[:], lk[:], ALU.mult, reads=[lq, lk], writes=[lq])
        ls = f.sb("at_ls", [128, 2], F32)
        f.op(f.dve, lambda: nc.vector.tensor_reduce(out=ls[:], in_=lq[:], axis=AX.X, op=ALU.add), reads=[lq], writes=[ls])
        f.actf(ls[:], ls[:], AF.Exp, reads=[ls], writes=[ls])
        neglam = f.sb("at_nl", [128, 1], F32)
        f.tt(neglam[:], ls[:, 1:2], ls[:, 0:1], ALU.subtract, reads=[ls], writes=[neglam])
        f.ts(neglam[:], neglam[:], -lam_init, None, ALU.add, reads=[neglam], writes=[neglam])
        dn = f.sb("at_dn", [128, 1], F32)
        load_featmajor(f, dn, env["diff_norm"].t[l:l + 1, :], 1, identf, psF)
        f.ts(dn[:], dn[:], 1.0 - lam_init, None, ALU.mult, reads=[dn], writes=[dn])
        sq = f.sb("at_sq", [128, 512], F32)
        rstd = f.sb("at_rstd", [128, 512], F32)
    si = 0
    pti = 0
    it = 0
    for h in range(8):
        ub = h % 2
        V = Vt[ub]
        vsrc = env["Va"] if kind == "mla" else env["Vd"]
        f.dma(V[:], vsrc.t[:, h * 128:(h + 1) * 128].rearrange("(n p) d -> p n d", p=128), reads=[vsrc], writes=[V])
        K = Kp[ub]
        if kind == "mla":
            f.dma(K[0][:], env["KaT"].t[h, :, :], reads=[env["KaT"]], writes=[K[0]])
            f.dma(K[1][:], env["KrT"].t[:, :], reads=[env["KrT"]], writes=[K[1]])
            kparts = [[(K[0], 128), (K[1], 64)]]
        else:
            for m in range(2):
                f.dma(K[m][:], env["KdT"].t[(2 * h + m) * 64:(2 * h + m + 1) * 64, :], reads=[env["KdT"]], writes=[K[m]])
            kparts = [[(K[0], 64)], [(K[1], 64)]]
        for (t0, nb) in BLOCKS:
            kts = [0, 1] if t0 < 256 else list(range(NT))
            Q = Qp[it % 2]
            g = gate[it % 2]
            y = yst[it % 2]
            it += 1
            if kind == "mla":
                f.dma(Q[0][:, 0:nb], env["QaT"].t[h, 0:128, t0:t0 + nb], reads=[env["QaT"]], writes=[Q[0]])
                f.dma(Q[1][:, 0:nb], env["QaT"].t[h, 128:192, t0:t0 + nb], reads=[env["QaT"]], writes=[Q[1]])
                qparts = [[(Q[0], 128), (Q[1], 64)]]
                gsrc = env["gaT"]
            else:
                for m in range(2):
                    f.dma(Q[m][:, 0:nb], env["QdT"].t[(2 * h + m) * 64:(2 * h + m + 1) * 64, t0:t0 + nb], reads=[env["QdT"]], writes=[Q[m]])
                qparts = [[(Q[0], 64)], [(Q[1], 64)]]
                gsrc = env["gdT"]
            f.dma(g[:, 0:nb], gsrc.t[h * 128:(h + 1) * 128, t0:t0 + nb], reads=[gsrc], writes=[g])
            for m in range(nmaps):
                for ki, kt in enumerate(kts):
                    S = psS[si % 2]; si += 1
                    np_ = len(kparts[m])
                    for pi_ in range(np_):
                        Kt, rows = kparts[m][pi_]
                        Qt, _ = qparts[m][pi_]
                        f.mm(S[:, 0:nb], Kt[0:rows, kt * 128:(kt + 1) * 128], Qt[0:rows, 0:nb], pi_ == 0, pi_ == np_ - 1, reads=[Kt, Qt], writes=[S])
                    P = Pt[pti % 3]; pti += 1
                    f.actf(P[:, 0:nb], S[:, 0:nb], AF.Exp, reads=[S], writes=[P], scale=scale)
                    f.mm(psO[m][:, 0:nb], V[:, kt, :], P[:, 0:nb], ki == 0, ki == len(kts) - 1, reads=[V, P], writes=[psO[m]])
                    f.mm(psM[m][:, 0:nb], ones_b[:], P[:, 0:nb], ki == 0, ki == len(kts) - 1, reads=[ones_b, P], writes=[psM[m]])
                f.op(f.dve, lambda m=m: nc.vector.reciprocal(out=rc[m][:, 0:nb], in_=psM[m][:, 0:nb]), reads=[psM[m]], writes=[rc[m]])
                f.tt(o[m][:, 0:nb], psO[m][:, 0:nb], rc[m][:, 0:nb], ALU.mult, reads=[psO[m], rc[m]], writes=[o[m]])
            if kind == "mla":
                f.tt(y[:, 0:nb], o[0][:, 0:nb], g[:, 0:nb], ALU.mult, reads=[o[0], g], writes=[y], q=f.pool)
                f.dma(env["YT"].t[0, h * 128:(h + 1) * 128, t0:t0 + nb], y[:, 0:nb], reads=[y], writes=[env["YT"]])
            else:
                f.stt(o[0][:, 0:nb], o[1][:, 0:nb], neglam[:, 0:1], o[0][:, 0:nb], ALU.mult, ALU.add, reads=[o[0], o[1], neglam], writes=[o[0]])
                f.tt(sq[:, 0:nb], o[0][:, 0:nb], o[0][:, 0:nb], ALU.mult, reads=[o[0]], writes=[sq], q=f.pool)
                f.mm(psF[:, 0:nb], ones_f[:], sq[:, 0:nb], True, True, reads=[ones_f, sq], writes=[psF])
                f.actf(rstd[:, 0:nb], psF[:, 0:nb], AF.Sqrt, reads=[psF, env["eps_rms"]], writes=[rstd], bias=env["eps_rms"][:, 0:1], scale=1.0 / 128)
                f.op(f.dve, lambda: nc.vector.reciprocal(out=rstd[:, 0:nb], in_=rstd[:, 0:nb]), reads=[rstd], writes=[rstd])
                f.tt(o[0][:, 0:nb], o[0][:, 0:nb], rstd[:, 0:nb], ALU.mult, reads=[o[0], rstd], writes=[o[0]])
                f.stt(y[:, 0:nb], o[0][:, 0:nb], dn[:, 0:1], g[:, 0:nb], ALU.mult, ALU.mult, reads=[o[0], dn, g], writes=[y])
                f.dma(env["YT"].t[3, h * 128:(h + 1) * 128, t0:t0 + nb], y[:, 0:nb], reads=[y], writes=[env["YT"]])
    f.release(m0)

import math

TWO_PI = 2.0 * math.pi
MAGIC = 12582912.0


def sin_reduced(f, out, x, tmp, shift=0.0):
    nc = f.nc
    f.ts(tmp[:], x[:], shift, 1.0 / TWO_PI, ALU.add, ALU.mult, reads=[x], writes=[tmp])
    f.ts(tmp[:], tmp[:], MAGIC, None, ALU.add, reads=[tmp], writes=[tmp])
    f.ts(tmp[:], tmp[:], -MAGIC, None, ALU.add, reads=[tmp], writes=[tmp])
    f.stt(out[:], tmp[:], -TWO_PI, x[:], ALU.mult, ALU.add, reads=[tmp, x], writes=[out])
    if shift != 0.0:
        f.ts(out[:], out[:], shift, None, ALU.add, reads=[out], writes=[out])
    f.ts(out[:], out[:], 3.14159, -3.14159, ALU.min, ALU.max, reads=[out], writes=[out])
    f.actf(out[:], out[:], AF.Sin, reads=[out], writes=[out])


def phase_s5(f, env, l):
    nc = f.nc
    m0 = f.mark()
    identf, ident = env["identf"], env["ident"]
    pst = f.ps("s5_pst", [128, 512], F32)
    TB = 64
    GPB = 512 // TB
    WBr = f.sb("s5_WBr", [16, 64, 128], BF16)
    WBi = f.sb("s5_WBi", [16, 64, 128], BF16)
    CrT = f.sb("s5_CrT", [128, 64, 16], BF16)
    CiN = f.sb("s5_CiN", [128, 64, 16], BF16)
    ar = f.sb("s5_ar", [128, 64], F32)
    ai = f.sb("s5_ai", [128, 64], F32)
    m1 = f.mark()
    def load_T(name, src):
        tmp = f.sb("s5_ld_" + name, [128, 128], F32)
        f.memset(tmp[:], 0.0, writes=[tmp])
        for d in range(2):
            f.dma(tmp[0:64, d * 64:(d + 1) * 64], src.t[l, d, :, :], writes=[tmp])
        dst = f.sb("s5_" + name, [128, 64], F32)
        f.tr(pst[:, 0:128], tmp[:, :], identf[:], reads=[tmp, identf], writes=[pst])
        f.cp(dst[:], pst[:, 0:64], reads=[pst], writes=[dst])
        return dst
    lr = load_T("lr", env["s5_lambda_re"])
    li = load_T("li", env["s5_lambda_im"])
    dt = f.sb("s5_dt", [128, 64], F32)
    for d in range(2):
        f.dma(dt[d * 64:(d + 1) * 64, :], env["s5_log_dt"].t[l, d:d + 1, :].partition_broadcast(64), writes=[dt])
    f.actf(dt[:], dt[:], AF.Exp, reads=[dt], writes=[dt])
    def tl(n):
        return f.sb("s5_" + n, [128, 64], F32)
    lrdt, lidt, mag, cs, sn, tmp = [tl(n) for n in ["lrdt", "lidt", "mag", "cs", "sn", "tmp"]]
    f.tt(lrdt[:], lr[:], dt[:], ALU.mult, reads=[lr, dt], writes=[lrdt])
    f.tt(lidt[:], li[:], dt[:], ALU.mult, reads=[li, dt], writes=[lidt])
    f.actf(mag[:], lrdt[:], AF.Exp, reads=[lrdt], writes=[mag])
    sin_reduced(f, sn, lidt, tmp, 0.0)
    sin_reduced(f, cs, lidt, tmp, math.pi / 2)
    f.tt(ar[:], mag[:], cs[:], ALU.mult, reads=[mag, cs], writes=[ar])
    f.tt(ai[:], mag[:], sn[:], ALU.mult, reads=[mag, sn], writes=[ai])
    den, am1, fre, fim, t1, t2 = [tl(n) for n in ["den", "am1", "fre", "fim", "t1", "t2"]]
    f.tt(t1[:], lr[:], lr[:], ALU.mult, reads=[lr], writes=[t1])
    f.tt(t2[:], li[:], li[:], ALU.mult, reads=[li], writes=[t2])
    f.tt(den[:], t1[:], t2[:], ALU.add, reads=[t1, t2], writes=[den])
    f.op(f.dve, lambda: nc.vector.reciprocal(out=den[:], in_=den[:]), reads=[den], writes=[den])
    f.ts(am1[:], ar[:], -1.0, None, ALU.add, reads=[ar], writes=[am1])
    f.tt(t1[:], am1[:], lr[:], ALU.mult, reads=[am1, lr], writes=[t1])
    f.tt(t2[:], ai[:], li[:], ALU.mult, reads=[ai, li], writes=[t2])
    f.tt(fre[:], t1[:], t2[:], ALU.add, reads=[t1, t2], writes=[fre])
    f.tt(fre[:], fre[:], den[:], ALU.mult, reads=[fre, den], writes=[fre])
    f.tt(t1[:], ai[:], lr[:], ALU.mult, reads=[ai, lr], writes=[t1])
    f.tt(t2[:], am1[:], li[:], ALU.mult, reads=[am1, li], writes=[t2])
    f.tt(fim[:], t1[:], t2[:], ALU.subtract, reads=[t1, t2], writes=[fim])
    f.tt(fim[:], fim[:], den[:], ALU.mult, reads=[fim, den], writes=[fim])
    Br = f.sb("s5_Br", [128, 64, 16], F32)
    Bi = f.sb("s5_Bi", [128, 64, 16], F32)
    for d in range(2):
        f.dma(Br[d * 64:(d + 1) * 64, :, :], env["s5_b_re"].t[l, d].rearrange("g p s -> p g s"), writes=[Br])
        f.dma(Bi[d * 64:(d + 1) * 64, :, :], env["s5_b_im"].t[l, d].rearrange("g p s -> p g s"), writes=[Bi])
    BBr = f.sb("s5_BBr", [128, 64, 16], F32)
    BBi = f.sb("s5_BBi", [128, 64, 16], F32)
    tb1 = f.sb("s5_tb1", [128, 64, 16], F32)
    def bc(t):
        return t[:].unsqueeze(2).to_broadcast([128, 64, 16])
    f.tt(BBr[:], Br[:], bc(fre), ALU.mult, reads=[Br, fre], writes=[BBr])
    f.tt(tb1[:], Bi[:], bc(fim), ALU.mult, reads=[Bi, fim], writes=[tb1])
    f.tt(BBr[:], BBr[:], tb1[:], ALU.subtract, reads=[BBr, tb1], writes=[BBr])
    f.tt(BBi[:], Bi[:], bc(fre), ALU.mult, reads=[Bi, fre], writes=[BBi])
    f.tt(tb1[:], Br[:], bc(fim), ALU.mult, reads=[Br, fim], writes=[tb1])
    f.tt(BBi[:], BBi[:], tb1[:], ALU.add, reads=[BBi, tb1], writes=[BBi])
    for (src, dst) in ((BBr, WBr), (BBi, WBi)):
        for g4 in range(16):
            for j in range(4):
                g = g4 * 4 + j
                f.tr(pst[0:16, j * 128:(j + 1) * 128], src[:, g, :], identf[:], reads=[src, identf], writes=[pst])
            f.cp(dst[:, g4 * 4:(g4 + 1) * 4, :], pst[0:16, :].rearrange("s (j q) -> s j q", q=128), reads=[pst], writes=[dst])
    ctmp = [f.sb(f"s5_ctmp{i}", [128, 128], F32) for i in range(2)]
    for (srcn, dst, sgn) in (("s5_c_re", CrT, 1.0), ("s5_c_im", CiN, -1.0)):
        for g8 in range(8):
            ct = ctmp[g8 % 2]
            for d in range(2):
                f.dma(ct[:, d * 64:(d + 1) * 64], env[srcn].t[l, d, g8 * 8:(g8 + 1) * 8, :, :].rearrange("g s p -> (g s) p"), writes=[ct])
            f.tr(pst[:, 0:128], ct[:, :], identf[:], reads=[ct, identf], writes=[pst])
            f.ts(dst[:, g8 * 8:(g8 + 1) * 8, :], pst[:, 0:128].rearrange("q (g s) -> q g s", s=16), sgn, None, ALU.mult, reads=[pst], writes=[dst])
    f.release(m1)
    psX = [[f.ps(f"s5_pX{c}{i}", [128, 512], F32) for i in range(2)] for c in range(2)]
    psY = [f.ps(f"s5_pY{i}", [16, 512], F32) for i in range(2)]
    Ufb = [f.sb(f"s5_Ufb{i}", [16, 64, TB], BF16) for i in range(2)]
    Ubb = [f.sb(f"s5_Ubb{i}", [16, 64, TB], BF16) for i in range(2)]
    Xr = [f.sb(f"s5_Xr{i}", [128, 64, TB], F32) for i in range(2)]
    Xi = [f.sb(f"s5_Xi{i}", [128, 64, TB], F32) for i in range(2)]
    Hrb = f.sb("s5_Hrb", [128, 64, TB], BF16)
    Hib = f.sb("s5_Hib", [128, 64, TB], BF16)
    yf = f.sb("s5_yf", [16, 64, TB], F32)
    yb = f.sb("s5_yb", [16, 64, TB], F32)
    tq = {}
    for qn in ("dve", "pool"):
        tq[qn] = [f.sb(f"s5_t{qn}{i}", [128, 32], F32) for i in range(4)]
    uT = env["uT"]
    prevX = None
    NBLK = T_ALL // TB

    def blo_of(k):
        n0 = TB * k
        return (255 - (n0 + TB - 1)) if n0 < 256 else (4544 - n0)

    def load_u(k):
        b = k % 2
        f.dma(Ufb[b][:], uT.t[:, TB * k:TB * k + TB].rearrange("(g s) t -> s g t", s=16), reads=[uT], writes=[Ufb[b]], q=f.pool)
        bl = blo_of(k)
        f.dma(Ubb[b][:], uT.t[:, bl:bl + TB].rearrange("(g s) t -> s g t", s=16), reads=[uT], writes=[Ubb[b]], q=f.pool)

    def inject(k):
        b = k % 2
        XR, XI = Xr[b], Xi[b]
        for gb_ in range(64 // GPB):
            pr = psX[0][gb_ % 2]
            pi_ = psX[1][gb_ % 2]
            for j in range(GPB):
                g = gb_ * GPB + j
                cs_ = slice(j * TB, (j + 1) * TB)
                f.mm(pr[0:64, cs_], WBr[:, g, 0:64], Ufb[b][:, g, :], True, True, reads=[WBr, Ufb[b]], writes=[pr])
                f.mm(pr[64:128, cs_], WBr[:, g, 64:128], Ubb[b][:, g, :], True, True, reads=[WBr, Ubb[b]], writes=[pr])
                f.mm(pi_[0:64, cs_], WBi[:, g, 0:64], Ufb[b][:, g, :], True, True, reads=[WBi, Ufb[b]], writes=[pi_])
                f.mm(pi_[64:128, cs_], WBi[:, g, 64:128], Ubb[b][:, g, :], True, True, reads=[WBi, Ubb[b]], writes=[pi_])
            gs = slice(gb_ * GPB, (gb_ + 1) * GPB)
            for (p_, X_) in ((pr, XR), (pi_, XI)):
                f.actf(X_[0:64, gs, :], p_[0:64, :].rearrange("p (j t) -> p j t", t=TB), AF.Copy, reads=[p_], writes=[X_])
                f.actf(X_[64:128, gs, :], p_[64:128, :].rearrange("p (j t) -> p j t", t=TB)[:, :, ::-1], AF.Copy, reads=[p_], writes=[X_])

    load_u(0)
    inject(0)
    for k in range(NBLK):
        b = k % 2
        XR, XI = Xr[b], Xi[b]
        if k + 1 < NBLK:
            load_u(k + 1)
        for tau in range(TB):
            for (qn, q, g0) in (("dve", f.dve, 0), ("pool", f.pool, 32)):
                gsl = slice(g0, g0 + 32)
                if tau == 0:
                    if prevX is None:
                        continue
                    hr_p = prevX[0][:, gsl, TB - 1]
                    hi_p = prevX[1][:, gsl, TB - 1]
                    rd = [prevX[0], prevX[1]]
                else:
                    hr_p = XR[:, gsl, tau - 1]
                    hi_p = XI[:, gsl, tau - 1]
                    rd = [XR, XI]
                t = tq[qn]
                f.tt(t[0][:], ar[:, gsl], hr_p, ALU.mult, reads=[ar] + rd, writes=[t[0]], q=q)
                f.tt(t[1][:], ai[:, gsl], hi_p, ALU.mult, reads=[ai] + rd, writes=[t[1]], q=q)
                f.tt(t[2][:], ai[:, gsl], hr_p, ALU.mult, reads=[ai] + rd, writes=[t[2]], q=q)
                f.tt(t[3][:], ar[:, gsl], hi_p, ALU.mult, reads=[ar] + rd, writes=[t[3]], q=q)
                f.tt(t[0][:], t[0][:], t[1][:], ALU.subtract, reads=[t[0], t[1]], writes=[t[0]], q=q)
                f.tt(t[2][:], t[2][:], t[3][:], ALU.add, reads=[t[2], t[3]], writes=[t[2]], q=q)
                f.tt(XR[:, gsl, tau], XR[:, gsl, tau], t[0][:], ALU.add, reads=[XR, t[0]], writes=[XR], q=q)
                f.tt(XI[:, gsl, tau], XI[:, gsl, tau], t[2][:], ALU.add, reads=[XI, t[2]], writes=[XI], q=q)
        prevX = (XR, XI)
        if k + 1 < NBLK:
            inject(k + 1)
        f.actf(Hrb[:], XR[:], AF.Copy, reads=[XR], writes=[Hrb])
        f.actf(Hib[:], XI[:], AF.Copy, reads=[XI], writes=[Hib])
        bl = blo_of(k)
        for d in range(2):
            ps_ = slice(d * 64, (d + 1) * 64)
            ydst = yf if d == 0 else yb
            for gb_ in range(64 // GPB):
                py = psY[gb_ % 2]
                for j in range(GPB):
                    g = gb_ * GPB + j
                    cs_ = slice(j * TB, (j + 1) * TB)
                    f.mm(py[0:16, cs_], CrT[ps_, g, :], Hrb[ps_, g, :], True, False, reads=[CrT, Hrb], writes=[py])
                    f.mm(py[0:16, cs_], CiN[ps_, g, :], Hib[ps_, g, :], False, True, reads=[CiN, Hib], writes=[py])
                src = py[0:16, :].rearrange("s (j t) -> s j t", t=TB)
                if d == 1:
                    src = src[:, :, ::-1]
                f.actf(ydst[:, gb_ * GPB:(gb_ + 1) * GPB, :], src, AF.Copy, reads=[py], writes=[ydst])
        f.dma(env["Yf"].t[:, TB * k:TB * k + TB].rearrange("(g s) t -> s g t", s=16), yf[:], reads=[yf], writes=[env["Yf"]])
        f.dma(env["Yb"].t[:, bl:bl + TB].rearrange("(g s) t -> s g t", s=16), yb[:], reads=[yb], writes=[env["Yb"]])
    f.release(m0)


def phase_s5_glu(f, env, l):
    nc = f.nc
    m0 = f.mark()
    identf = env["identf"]
    pst = f.ps("g5_pst", [128, 512], F32)
    pss = [f.ps(f"g5_ps{i}", [128, 512], F32) for i in range(2)]
    dvec = f.sb("g5_d", [128, 8], F32)
    bglu = f.sb("g5_b", [128, 8], F32)
    load_featmajor(f, dvec, env["s5_d"].t[l].rearrange("(k p) -> k p", p=128), 8, identf, pst)
    load_featmajor(f, bglu, env["s5_b_glu"].t[l].rearrange("(k p) -> k p", p=128), 8, identf, pst)
    wg = f.sb("g5_w", [128, 8, 1024], BF16)
    f.dma(wg[:], env["s5_w_glu"].t[l].rearrange("(k p) c -> p k c", p=128), writes=[wg], q=f.pool)
    u = [f.sb(f"g5_u{i}", [128, 8, 512], F32) for i in range(2)]
    ya = [f.sb(f"g5_ya{i}", [128, 8, 512], F32) for i in range(2)]
    yb = [f.sb(f"g5_yb{i}", [128, 8, 512], F32) for i in range(2)]
    gg = f.sb("g5_g", [128, 8, 512], F32)
    gb16 = f.sb("g5_g16", [128, 8, 512], BF16)
    t1 = f.sb("g5_t1", [128, 8, 512], F32)
    gate = [f.sb(f"g5_gate{i}", [128, 512], BF16) for i in range(2)]
    sg = [f.sb(f"g5_sg{i}", [128, 512], F32) for i in range(2)]
    yo = [f.sb(f"g5_yo{i}", [128, 512], BF16) for i in range(2)]
    it = 0
    for bi, (t0, nb) in enumerate(BLOCKS):
        b = bi % 2
        for (dst, src) in ((u[b], env["uT"]), (ya[b], env["Yf"]), (yb[b], env["Yb"])):
            f.dma(dst[:, :, 0:nb], src.t[:, t0:t0 + nb].rearrange("(k p) t -> p k t", p=128), reads=[src], writes=[dst])
        f.tt(ya[b][:, :, 0:nb], ya[b][:, :, 0:nb], yb[b][:, :, 0:nb], ALU.add, reads=[ya[b], yb[b]], writes=[ya[b]], q=f.pool)
        for k in range(8):
            f.stt(gg[:, k, 0:nb], u[b][:, k, 0:nb], dvec[:, k:k + 1], ya[b][:, k, 0:nb], ALU.mult, ALU.add, reads=[u[b], dvec, ya[b]], writes=[gg])
        f.tt(t1[:, :, 0:nb], gg[:, :, 0:nb], gg[:, :, 0:nb], ALU.mult, reads=[gg], writes=[t1])
        f.ts(t1[:, :, 0:nb], t1[:, :, 0:nb], 0.044715, 1.0, ALU.mult, ALU.add, reads=[t1], writes=[t1])
        f.tt(t1[:, :, 0:nb], t1[:, :, 0:nb], gg[:, :, 0:nb], ALU.mult, reads=[t1, gg], writes=[t1])
        f.actf(t1[:, :, 0:nb], t1[:, :, 0:nb], AF.Sigmoid, reads=[t1], writes=[t1], scale=1.5957691216057308)
        f.tt(gg[:, :, 0:nb], gg[:, :, 0:nb], t1[:, :, 0:nb], ALU.mult, reads=[gg, t1], writes=[gg])
        f.cp(gb16[:, :, 0:nb], gg[:, :, 0:nb], reads=[gg], writes=[gb16], q=f.pool)
        for ct in range(8):
            p = pss[it % 2]
            s_ = sg[it % 2]
            g_ = gate[it % 2]
            y_ = yo[it % 2]
            it += 1
            f.dma(g_[:, 0:nb], env["gbT"].t[ct * 128:(ct + 1) * 128, t0:t0 + nb], reads=[env["gbT"]], writes=[g_])
            for k in range(8):
                f.mm(p[:, 0:nb], wg[:, k, ct * 128:(ct + 1) * 128], gb16[:, k, 0:nb], k == 0, k == 7, reads=[wg, gb16], writes=[p])
            f.actf(s_[:, 0:nb], p[:, 0:nb], AF.Sigmoid, reads=[p, bglu], writes=[s_], bias=bglu[:, ct:ct + 1], scale=1.0)
            f.tt(s_[:, 0:nb], s_[:, 0:nb], gg[:, ct, 0:nb], ALU.mult, reads=[s_, gg], writes=[s_])
            f.tt(y_[:, 0:nb], s_[:, 0:nb], g_[:, 0:nb], ALU.mult, reads=[s_, g_], writes=[y_])
            f.dma(env["YT"].t[1, ct * 128:(ct + 1) * 128, t0:t0 + nb], y_[:, 0:nb], reads=[y_], writes=[env["YT"]])
    f.release(m0)


NEG = -30000.0


def phase_ssd_prep(f, env, l):
    nc = f.nc
    m0 = f.mark()
    identf, ident = env["identf"], env["ident"]
    pst = f.ps("sp_pst", [128, 512], F32)
    pstb = f.ps("sp_pstb", [128, 1024], BF16)
    cw = [f.sb(f"sp_cw{i}", [128, 16], F32) for i in range(3)]
    cb = f.sb("sp_cb", [128, 16], F32)
    for i in range(3):
        load_featmajor(f, cw[i], env["ssd_conv_w"].t[l, i].rearrange("(k p) -> k p", p=128), 16, identf, pst)
    load_featmajor(f, cb, env["ssd_conv_b"].t[l].rearrange("(k p) -> k p", p=128), 16, identf, pst)
    xin = [f.sb(f"sp_x{i}", [128, T_ALL], F32) for i in range(2)]
    xo = [f.sb(f"sp_o{i}", [128, T_ALL], F32) for i in range(2)]
    xob = f.sb("sp_ob", [128, T_ALL], BF16)
    stg = [f.sb(f"sp_s{i}", [128, 4, 128], F32) for i in range(2)]
    stgb = [f.sb(f"sp_sb{i}", [128, 8, 128], BF16) for i in range(2)]
    src = env["xbcT"]
    si = 0
    for kt in range(16):
        b = kt % 2
        x, o = xin[b], xo[b]
        f.dma(x[:], src.t[kt * 128:(kt + 1) * 128, :], reads=[src], writes=[x])
        for (a, e) in ((0, 256), (256, T_ALL)):
            f.ts(o[:, a:e], x[:, a:e], cw[1][:, kt:kt + 1], None, ALU.mult, reads=[x, cw[1]], writes=[o])
            f.stt(o[:, a + 1:e], x[:, a:e - 1], cw[0][:, kt:kt + 1], o[:, a + 1:e], ALU.mult, ALU.add, reads=[x, cw[0], o], writes=[o])
            f.stt(o[:, a:e - 1], x[:, a + 1:e], cw[2][:, kt:kt + 1], o[:, a:e - 1], ALU.mult, ALU.add, reads=[x, cw[2], o], writes=[o])
        f.actf(o[:], o[:], AF.Silu, reads=[o, cb], writes=[o], bias=cb[:, kt:kt + 1], scale=1.0)
        if kt < 8:
            f.dma(env["XcT"].t[kt * 128:(kt + 1) * 128, :], o[:], reads=[o], writes=[env["XcT"]])
            for t4 in range(0, NT, 4):
                n = min(4, NT - t4)
                for j in range(n):
                    f.tr(pst[:, j * 128:(j + 1) * 128], o[:, (t4 + j) * 128:(t4 + j + 1) * 128], identf[:], reads=[o, identf], writes=[pst])
                s = stg[si % 2]; si += 1
                f.cp(s[:, 0:n, :], pst[:, 0:n * 128].rearrange("p (n c) -> p n c", c=128), reads=[pst], writes=[s])
                f.dma(env["Xtok"].t[t4 * 128:(t4 + n) * 128, kt * 128:(kt + 1) * 128].rearrange("(n p) c -> p n c", p=128), s[:, 0:n, :], reads=[s], writes=[env["Xtok"]])
        else:
            f.cp(xob[:], o[:], reads=[o], writes=[xob], q=f.pool)
            if kt < 12:
                g = kt - 8
                f.dma(env["BT"].t[g * 128:(g + 1) * 128, :], xob[:], reads=[xob], writes=[env["BT"]])
                for t8 in range(0, NT, 8):
                    n = min(8, NT - t8)
                    for j in range(n):
                        f.tr(pstb[:, j * 128:(j + 1) * 128], xob[:, (t8 + j) * 128:(t8 + j + 1) * 128], ident[:], reads=[xob, ident], writes=[pstb])
                    s = stgb[si % 2]; si += 1
                    f.cp(s[:, 0:n, :], pstb[:, 0:n * 128].rearrange("p (n c) -> p n c", c=128), reads=[pstb], writes=[s])
                    f.dma(env["Btok"].t[t8 * 128:(t8 + n) * 128, g * 128:(g + 1) * 128].rearrange("(n p) c -> p n c", p=128), s[:, 0:n, :], reads=[s], writes=[env["Btok"]])
            else:
                g = kt - 12
                f.dma(env["CT"].t[g * 128:(g + 1) * 128, :], xob[:], reads=[xob], writes=[env["CT"]])
    f.release(m0)


def phase_ssd_scan(f, env, l):
    nc = f.nc
    m0 = f.mark()
    ones_f = env["ones_f"]
    triF = f.sb("ss_triF", [128, 128], F32)
    triB = f.sb("ss_triB", [128, 128], F32)
    negF = f.sb("ss_negF", [128, 128], F32)
    negB = f.sb("ss_negB", [128, 128], F32)
    for (t_, cmp_, fillv, base) in ((triF, None, None, None),):
        pass
    f.memset(triF[:], 1.0, writes=[triF], q=f.pool)
    f.op(f.pool, lambda: nc.gpsimd.affine_select(out=triF[:], in_=triF[:], pattern=[[1, 128]], compare_op=ALU.is_ge, fill=0.0, base=0, channel_multiplier=-1), reads=[triF], writes=[triF])
    f.memset(triB[:], 1.0, writes=[triB], q=f.pool)
    f.op(f.pool, lambda: nc.gpsimd.affine_select(out=triB[:], in_=triB[:], pattern=[[-1, 128]], compare_op=ALU.is_ge, fill=0.0, base=0, channel_multiplier=1), reads=[triB], writes=[triB])
    f.ts(negF[:], triF[:], -1.0, -NEG, ALU.add, ALU.mult, reads=[triF], writes=[negF])
    f.ts(negB[:], triB[:], -1.0, -NEG, ALU.add, ALU.mult, reads=[triB], writes=[negB])
    dtb = f.sb("ss_dtb", [128, 2, 16], F32)
    aneg = f.sb("ss_a", [128, 2, 16], F32)
    f.dma(dtb[:].rearrange("p a h -> p (a h)"), env["ssd_dt_bias"].t[l:l + 1].rearrange("o a h -> o (a h)").partition_broadcast(128), writes=[dtb])
    f.dma(aneg[:].rearrange("p a h -> p (a h)"), env["ssd_a_log"].t[l:l + 1].rearrange("o a h -> o (a h)").partition_broadcast(128), writes=[aneg])
    f.actf(aneg[:], aneg[:], AF.Exp, reads=[aneg], writes=[aneg])
    f.ts(aneg[:], aneg[:], -1.0, None, ALU.mult, reads=[aneg], writes=[aneg])
    ps_cum = f.ps("ss_pcum", [128, 512], F32)
    ps_cr = [f.ps(f"ss_pcr{i}", [128, 512], F32) for i in range(2)]
    ps_cb = f.ps("ss_pcb", [128, 512], F32)
    ps_y = [f.ps(f"ss_py{i}", [128, 512], F32) for i in range(2)]
    ps_s = f.ps("ss_ps", [128, 512], F32)
    xt = [f.sb(f"ss_x{i}", [128, 16, 64], F32) for i in range(2)]
    dtr = [f.sb(f"ss_dtr{i}", [128, 16], F32) for i in range(2)]
    btok = [f.sb(f"ss_bt{i}", [128, 512], BF16) for i in range(2)]
    bT = [f.sb(f"ss_bT{i}", [128, 4, 128], BF16) for i in range(2)]
    cT = [f.sb(f"ss_cT{i}", [128, 4, 128], BF16) for i in range(2)]
    sm = [f.sb(f"ss_sm{i}", [128, 16], F32) for i in range(8)]
    xdt = f.sb("ss_xdt", [128, 16, 64], BF16)
    xte = f.sb("ss_xte", [128, 16, 64], BF16)
    trida = [f.sb(f"ss_td{i}", [128, 128], F32) for i in range(2)]
    arg = [f.sb(f"ss_arg{i}", [128, 128], F32) for i in range(2)]
    E = [f.sb(f"ss_E{i}", [128, 128], F32) for i in range(2)]
    cbs = f.sb("ss_cbs", [128, 4, 128], F32)
    M = [f.sb(f"ss_M{i}", [128, 128], BF16) for i in range(3)]
    CE = [f.sb(f"ss_CE{i}", [128, 128], BF16) for i in range(3)]
    H = f.sb("ss_H", [128, 16, 64], F32)
    Hb = f.sb("ss_Hb", [128, 16, 64], BF16)
    yst = [f.sb(f"ss_yst{i}", [128, 8, 128], F32) for i in range(2)]
    it = 0
    for d in range(2):
        tri = triF if d == 0 else triB
        neg = negF if d == 0 else negB
        last = 127 if d == 0 else 0
        order = [0, 1] + list(range(2, NT)) if d == 0 else [1, 0] + list(range(NT - 1, 1, -1))
        f.memset(H[:], 0.0, writes=[H])
        f.memset(Hb[:], 0.0, writes=[Hb])
        for c in order:
            b = it % 2
            it += 1
            t0 = c * 128
            X, DT, BK, BTt, CTt = xt[b], dtr[b], btok[b], bT[b], cT[b]
            f.dma(X[:].rearrange("p h d -> p (h d)"), env["Xtok"].t[t0:t0 + 128, :], reads=[env["Xtok"]], writes=[X])
            f.dma(DT[:], env["dtraw"].t[t0:t0 + 128, :], reads=[env["dtraw"]], writes=[DT])
            f.dma(BK[:], env["Btok"].t[t0:t0 + 128, :], reads=[env["Btok"]], writes=[BK])
            f.dma(BTt[:], env["BT"].t[:, t0:t0 + 128].rearrange("(g n) t -> n g t", n=128), reads=[env["BT"]], writes=[BTt])
            f.dma(CTt[:], env["CT"].t[:, t0:t0 + 128].rearrange("(g n) t -> n g t", n=128), reads=[env["CT"]], writes=[CTt])
            xv, ax, ex, dts, da, cum, cl, te = sm
            f.tt(xv[:], DT[:], dtb[:, d, :], ALU.add, reads=[DT, dtb], writes=[xv])
            f.stt(ax[:], xv[:], -1.0, xv[:], ALU.mult, ALU.max, reads=[xv], writes=[ax])
            f.actf(ex[:], ax[:], AF.Exp, reads=[ax], writes=[ex], scale=-1.0)
            f.actf(ex[:], ex[:], AF.Ln, reads=[ex], writes=[ex], bias=1.0, scale=1.0)
            f.ts(xv[:], xv[:], 0.0, None, ALU.max, reads=[xv], writes=[xv])
            f.tt(dts[:], xv[:], ex[:], ALU.add, reads=[xv, ex], writes=[dts])
            f.tt(da[:], dts[:], aneg[:, d, :], ALU.mult, reads=[dts, aneg], writes=[da])
            f.mm(ps_cum[:, 0:16], tri[:], da[:], True, True, reads=[tri, da], writes=[ps_cum])
            f.cp(cum[:], ps_cum[:, 0:16], reads=[ps_cum], writes=[cum])
            f.tt(xdt[:], X[:], dts[:].unsqueeze(2).to_broadcast([128, 16, 64]), ALU.mult, reads=[X, dts], writes=[xdt])
            for g in range(4):
                f.mm(ps_cb[:, g * 128:(g + 1) * 128], BTt[:, g, :], CTt[:, g, :], True, True, reads=[BTt, CTt], writes=[ps_cb])
            f.actf(cbs[:].rearrange("p g t -> p (g t)"), ps_cb[:], AF.Copy, reads=[ps_cb], writes=[cbs])
            ylist = []
            for h in range(16):
                g = h // 4
                pcr = ps_cr[(h // 4) % 2]
                hc = slice((h % 4) * 128, (h % 4 + 1) * 128)
                td = trida[h % 2]
                f.ts(td[:], tri[:], da[:, h:h + 1], None, ALU.mult, reads=[tri, da], writes=[td], q=f.pool)
                f.mm(pcr[:, hc], ones_f[:], td[:], True, True, reads=[ones_f, td], writes=[pcr])
                f.cp(cl[:, h:h + 1], pcr[:, (h % 4) * 128 + last:(h % 4) * 128 + last + 1], reads=[pcr], writes=[cl])
                ag = arg[h % 2]
                f.stt(ag[:], pcr[:, hc], cum[:, h:h + 1], neg[:], ALU.subtract, ALU.add, reads=[pcr, cum, neg], writes=[ag])
                f.actf(ag[:], ag[:], AF.Exp, reads=[ag], writes=[ag])
                m_ = M[h % 3]
                f.tt(m_[:], ag[:], cbs[:, g, :], ALU.mult, reads=[ag, cbs], writes=[m_])
                e_ = E[h % 2]
                f.actf(e_[:], pcr[:, hc], AF.Exp, reads=[pcr], writes=[e_])
                ce = CE[h % 3]
                f.tt(ce[:], e_[:], CTt[:, g, :], ALU.mult, reads=[e_, CTt], writes=[ce], q=f.pool)
                py = ps_y[(h // 2) // 4]
                po = slice((h % 2) * 64, (h % 2) * 64 + 64)
                yc = slice(((h // 2) % 4) * 128, ((h // 2) % 4 + 1) * 128)
                f.mm(py[po, yc], xdt[:, h, :], m_[:], True, False, reads=[xdt, m_], writes=[py])
                f.mm(py[po, yc], Hb[:, h, :], ce[:], False, True, reads=[Hb, ce], writes=[py])
            ys = yst[b]
            for i in range(2):
                f.actf(ys[:, i * 4:(i + 1) * 4, :].rearrange("p k t -> p (k t)"), ps_y[i][:], AF.Copy, reads=[ps_y[i]], writes=[ys])
            f.dma(env["YsT"].t[d, :, t0:t0 + 128].rearrange("(k p) t -> p k t", p=128), ys[:], reads=[ys], writes=[env["YsT"]])
            f.tt(te[:], cl[:], cum[:], ALU.subtract, reads=[cl, cum], writes=[te])
            f.actf(te[:], te[:], AF.Exp, reads=[te], writes=[te])
            f.actf(cl[:], cl[:], AF.Exp, reads=[cl], writes=[cl])
            f.tt(xte[:], xdt[:], te[:].unsqueeze(2).to_broadcast([128, 16, 64]), ALU.mult, reads=[xdt, te], writes=[xte])
            for g2 in range(2):
                for gg in range(2):
                    g = g2 * 2 + gg
                    f.mm(ps_s[:, gg * 256:(gg + 1) * 256], BK[:, g * 128:(g + 1) * 128], xte[:, g * 4:(g + 1) * 4, :].rearrange("p h d -> p (h d)"), True, True, reads=[BK, xte], writes=[ps_s])
                for hh in range(8):
                    h = g2 * 8 + hh
                    f.stt(H[:, h, :], H[:, h, :], cl[:, h:h + 1], ps_s[:, hh * 64:(hh + 1) * 64], ALU.mult, ALU.add, reads=[H, cl, ps_s], writes=[H])
            f.cp(Hb[:], H[:], reads=[H], writes=[Hb], q=f.pool)
    f.release(m0)


def phase_ssd_fin(f, env, l):
    nc = f.nc
    m0 = f.mark()
    identf = env["identf"]
    pst = f.ps("sf_pst", [128, 512], F32)
    ps_s = f.ps("sf_pss", [128, 512], F32)
    ng = f.sb("sf_ng", [128, 8], F32)
    load_featmajor(f, ng, env["ssd_norm"].t[l].rearrange("(k p) -> k p", p=128), 8, identf, pst)
    dfm = f.sb("sf_d", [128, 8], F32)
    dv = env["ssd_d"].t[l:l + 1, :].rearrange("o (k two) -> o two k", two=2)
    for hf in range(2):
        f.dma(dfm[hf * 64:(hf + 1) * 64, :], dv[:, hf, :].partition_broadcast(64), writes=[dfm], allow_slow_non_contiguous=True)
    xc = [f.sb(f"sf_x{i}", [128, 8, 512], F32) for i in range(2)]
    ya = [f.sb(f"sf_ya{i}", [128, 8, 512], F32) for i in range(2)]
    yb = [f.sb(f"sf_yb{i}", [128, 8, 512], F32) for i in range(2)]
    z = [f.sb(f"sf_z{i}", [128, 8, 512], BF16) for i in range(2)]
    sq = f.sb("sf_sq", [128, 8, 512], F32)
    rstd = f.sb("sf_rstd", [128, 512], F32)
    yo = [f.sb(f"sf_yo{i}", [128, 8, 512], BF16) for i in range(2)]
    for bi, (t0, nb) in enumerate(BLOCKS):
        b = bi % 2
        for (dst, src) in ((xc[b], env["XcT"].t), (ya[b], env["YsT"].t[0]), (yb[b], env["YsT"].t[1]), (z[b], env["zT"].t)):
            srcT = env["XcT"] if dst is xc[b] else (env["zT"] if dst is z[b] else env["YsT"])
            f.dma(dst[:, :, 0:nb], src[:, t0:t0 + nb].rearrange("(k p) t -> p k t", p=128), reads=[srcT], writes=[dst])
        f.tt(ya[b][:, :, 0:nb], ya[b][:, :, 0:nb], yb[b][:, :, 0:nb], ALU.add, reads=[ya[b], yb[b]], writes=[ya[b]], q=f.pool)
        for k in range(8):
            f.stt(ya[b][:, k, 0:nb], xc[b][:, k, 0:nb], dfm[:, k:k + 1], ya[b][:, k, 0:nb], ALU.mult, ALU.add, reads=[xc[b], dfm, ya[b]], writes=[ya[b]])
        f.tt(ya[b][:, :, 0:nb], ya[b][:, :, 0:nb], z[b][:, :, 0:nb], ALU.mult, reads=[ya[b], z[b]], writes=[ya[b]])
        rms_rstd(f, env, ya[b], 8, 1024, nb, sq, ps_s, rstd)
        for k in range(8):
            f.stt(yo[b][:, k, 0:nb], ya[b][:, k, 0:nb], ng[:, k:k + 1], rstd[:, 0:nb], ALU.mult, ALU.mult, reads=[ya[b], ng, rstd], writes=[yo[b]])
        f.dma(env["YT"].t[2, :, t0:t0 + nb].rearrange("(k p) t -> p k t", p=128), yo[b][:, :, 0:nb], reads=[yo[b]], writes=[env["YT"]])
    f.release(m0)


ALPHA = (2 * 2) ** 0.25


def phase_merge(f, env, l, blocks=None):
    nc = f.nc
    m0 = f.mark()
    blocks = blocks or BLOCKS
    pss = [f.ps(f"mg_ps{i}", [128, 512], F32) for i in range(4)]
    wb = [f.sb(f"mg_w{i}", [128, 8, 1024], BF16) for i in range(4)]
    yb = [f.sb(f"mg_y{i}", [128, 8, 512], BF16) for i in range(2)]
    mF = f.sb("mg_mF", [128, 16, 512], F32)
    mB = f.sb("mg_mB", [128, 16, 512], BF16)
    gt = [f.sb(f"mg_g{i}", [128, 512], BF16) for i in range(3)]
    tmp = [f.sb(f"mg_t{i}", [128, 512], F32) for i in range(2)]
    wi = 0
    yi = 0
    gi = 0
    pi = 0
    for (t0, nb) in blocks:
        for n in range(4):
            y = yb[yi % 2]; yi += 1
            f.dma(y[:, :, 0:nb], env["YT"].t[n, :, t0:t0 + nb].rearrange("(k p) t -> p k t", p=128), reads=[env["YT"]], writes=[y])
            for hf in range(2):
                w = wb[wi % 4]; wi += 1
                f.dma(w[:], env["w_branch"].t[l, n, :, hf * 1024:(hf + 1) * 1024].rearrange("(k p) c -> p k c", p=128), writes=[w], q=f.pool)
                for d8 in range(8):
                    dt_ = hf * 8 + d8
                    p = pss[pi % 4]; pi += 1
                    for k in range(8):
                        f.mm(p[:, 0:nb], w[:, k, d8 * 128:(d8 + 1) * 128], y[:, k, 0:nb], k == 0, k == 7, reads=[w, y], writes=[p])
                    g = gt[gi % 3]; gi += 1
                    r0 = n * 2048 + dt_ * 128
                    f.dma(g[:, 0:nb], env["gmT"].t[r0:r0 + 128, t0:t0 + nb], reads=[env["gmT"]], writes=[g])
                    if n == 0:
                        f.tt(mF[:, dt_, 0:nb], p[:, 0:nb], g[:, 0:nb], ALU.mult, reads=[p, g], writes=[mF])
                    else:
                        tq = tmp[pi % 2]
                        f.tt(tq[:, 0:nb], p[:, 0:nb], g[:, 0:nb], ALU.mult, reads=[p, g], writes=[tq])
                        f.tt(mF[:, dt_, 0:nb], mF[:, dt_, 0:nb], tq[:, 0:nb], ALU.add, reads=[mF, tq], writes=[mF], q=f.pool)
        f.actf(mB[:, :, 0:nb], mF[:, :, 0:nb], AF.Copy, reads=[mF], writes=[mB])
        f.dma(env["mergedT"].t[:, t0:t0 + nb].rearrange("(k p) t -> p k t", p=128), mB[:, :, 0:nb], reads=[mB], writes=[env["mergedT"]])
    f.release(m0)


def phase_out(f, env, l, xsrc, xdst, dst_off, tiles=None):
    nc = f.nc
    m0 = f.mark()
    tiles = tiles if tiles is not None else list(range(NT))
    pss = [f.ps(f"po_ps{i}", [128, 512], F32) for i in range(4)]
    wo = f.sb("po_w", [128, 16, 2048], BF16)
    for hf in range(2):
        f.dma(wo[:, :, hf * 1024:(hf + 1) * 1024], env["w_out"].t[l, :, hf * 1024:(hf + 1) * 1024].rearrange("(k p) c -> p k c", p=128), writes=[wo], q=f.pool)
    gL = f.sb("po_gL", [128, D], F32)
    gC = f.sb("po_gC", [128, D], F32)
    lg = f.sb("po_lg", [128, D], F32)
    lb = f.sb("po_lb", [128, D], F32)
    f.dma(gL[:], env["modrows"].t[2:3, :].partition_broadcast(128), reads=[env["modrows"]], writes=[gL])
    f.dma(gC[:], env["modrows"].t[5:6, :].partition_broadcast(128), reads=[env["modrows"]], writes=[gC])
    f.dma(lg[:], env["ln_g"].t[l:l + 1, :].partition_broadcast(128), writes=[lg])
    f.dma(lb[:], env["ln_b"].t[l:l + 1, :].partition_broadcast(128), writes=[lb])
    mt = [f.sb(f"po_m{i}", [128, 16, 128], BF16) for i in range(2)]
    xt = [f.sb(f"po_x{i}", [128, D], F32) for i in range(2)]
    rt = [f.sb(f"po_r{i}", [128, D], F32) for i in range(2)]
    st = [f.sb(f"po_s{i}", [128, 4, 6], F32) for i in range(2)]
    mv = [f.sb(f"po_mv{i}", [128, 2], F32) for i in range(2)]
    rs = [f.sb(f"po_rs{i}", [128, 1], F32) for i in range(2)]
    epst = f.sb("po_eps", [128, 1], F32)
    f.memset(epst[:], LN_EPS, writes=[epst])
    pi = 0
    for ii, i in enumerate(tiles):
        t0 = i * 128
        if t0 < dst_off:
            continue
        b = ii % 2
        gate = gC if t0 < 256 else gL
        f.dma(mt[b][:], env["mergedT"].t[:, t0:t0 + 128].rearrange("(k p) t -> p k t", p=128), reads=[env["mergedT"]], writes=[mt[b]])
        f.dma(xt[b][:], xsrc.t[t0:t0 + 128, :], reads=[xsrc], writes=[xt[b]])
        for cc in range(4):
            p = pss[pi % 4]; pi += 1
            for k in range(16):
                f.mm(p[:, :], mt[b][:, k, :], wo[:, k, cc * 512:(cc + 1) * 512], k == 0, k == 15, reads=[mt[b], wo], writes=[p])
            f.tt(rt[b][:, cc * 512:(cc + 1) * 512], p[:, :], gate[:, cc * 512:(cc + 1) * 512], ALU.mult, reads=[p, gate], writes=[rt[b]])
        f.stt(rt[b][:], xt[b][:], ALPHA, rt[b][:], ALU.mult, ALU.add, reads=[xt[b], rt[b]], writes=[rt[b]])
        for c in range(4):
            f.op(f.dve, lambda c=c: nc.vector.bn_stats(out=st[b][:, c, :], in_=rt[b][:, c * 512:(c + 1) * 512]), reads=[rt[b]], writes=[st[b]])
        f.op(f.dve, lambda: nc.vector.bn_aggr(out=mv[b][:], in_=st[b][:].rearrange("p a s -> p (a s)")), reads=[st[b]], writes=[mv[b]])
        f.actf(rs[b][:], mv[b][:, 1:2], AF.Sqrt, reads=[mv[b], epst], writes=[rs[b]], bias=epst[:, 0:1], scale=1.0)
        f.op(f.dve, lambda: nc.vector.reciprocal(out=rs[b][:], in_=rs[b][:]), reads=[rs[b]], writes=[rs[b]])
        f.ts(rt[b][:], rt[b][:], mv[b][:, 0:1], rs[b][:, 0:1], ALU.subtract, ALU.mult, reads=[rt[b], mv[b], rs[b]], writes=[rt[b]])
        f.tt(rt[b][:], rt[b][:], lg[:], ALU.mult, reads=[rt[b], lg], writes=[rt[b]], q=f.pool)
        f.tt(rt[b][:], rt[b][:], lb[:], ALU.add, reads=[rt[b], lb], writes=[rt[b]], q=f.pool)
        f.dma(xdst.t[t0 - dst_off:t0 - dst_off + 128, :], rt[b][:], reads=[rt[b]], writes=[xdst])
    f.release(m0)


NCORES = 4
WEIGHT_SHAPES = {
    "w_ada": [2, 2048, 6144], "b_ada": [2, 6144], "w_in": [2, 2048, 19280], "mla_q_norm": [2, 512], "mla_w_uq": [2, 512, 1536],
    "mla_kv_norm": [2, 256], "mla_w_ukv": [2, 256, 2048], "s5_lambda_re": [2, 2, 64, 64], "s5_lambda_im": [2, 2, 64, 64],
    "s5_log_dt": [2, 2, 64], "s5_b_re": [2, 2, 64, 64, 16], "s5_b_im": [2, 2, 64, 64, 16], "s5_c_re": [2, 2, 64, 16, 64],
    "s5_c_im": [2, 2, 64, 16, 64], "s5_d": [2, 1024], "s5_w_glu": [2, 1024, 1024], "s5_b_glu": [2, 1024],
    "ssd_conv_w": [2, 3, 2048], "ssd_conv_b": [2, 2048], "ssd_dt_bias": [2, 2, 16], "ssd_a_log": [2, 2, 16], "ssd_d": [2, 16],
    "ssd_norm": [2, 1024], "diff_lambda_q": [2, 2, 64], "diff_lambda_k": [2, 2, 64], "diff_norm": [2, 128],
    "b_merge": [2, 4, 2048], "w_branch": [2, 4, 1024, 2048], "w_out": [2, 2048, 2048], "ln_g": [2, 2048], "ln_b": [2, 2048],
}


def rope_tables_host():
    quarter = 16
    inv = (10000.0 ** (-np.arange(quarter, dtype=np.float32) / quarter)).astype(np.float32)
    t = np.arange(4096)
    row = (t // 64).astype(np.float32)
    col = (t % 64).astype(np.float32)
    ang = np.concatenate([row[:, None] * inv, col[:, None] * inv], -1).astype(np.float32)
    c = np.cos(ang).astype(np.float32).T
    s = np.sin(ang).astype(np.float32).T
    C2 = np.concatenate([c, c, c, c], 0)
    S2 = np.concatenate([-s, s, -s, s], 0)
    return np.ascontiguousarray(C2), np.ascontiguousarray(S2)


def build_program(nb=1, depth=2):
    nc = bass.Bass("TRN2", target_bir_lowering=False)
    f = FW(nc)
    env = {}
    T = T_ALL

    def din(n, shape, dt):
        env[n] = f.dram(n, shape, dt, kind="ExternalInput")

    def dint(n, shape, dt):
        env[n] = f.dram(n, shape, dt, kind="Internal")

    for n, s in WEIGHT_SHAPES.items():
        din(n, s, F32)
    din("ropeC", [128, 4096], F32)
    din("ropeS", [128, 4096], F32)
    din("c_ctx", [1, D], F32)
    xins, cs, outs = [], [], []
    for bb in range(nb):
        xins.append(f.dram(f"x_all{bb}", [T, D], F32, kind="ExternalInput"))
        cs.append(f.dram(f"c{bb}", [1, D], F32, kind="ExternalInput"))
        outs.append(f.dram(f"out{bb}", [4096, D], F32, kind="ExternalOutput"))
    dint("modrows", [6, D], F32)
    dint("hT", [16, 128, T], BF16)
    for n, w, dt in [("cqT", 512, F32), ("ckvT", 256, F32), ("krT", 64, F32), ("gaT", 1024, BF16), ("uT", 1024, F32), ("gbT", 1024, BF16),
                     ("zT", 1024, BF16), ("xbcT", 2048, F32), ("qdT", 1024, F32), ("kdT", 1024, F32), ("gdT", 1024, BF16), ("gmT", 8192, BF16),
                     ("KrT", 64, BF16), ("QdT", 1024, BF16), ("KdT", 1024, BF16), ("Yf", 1024, F32), ("Yb", 1024, F32), ("XcT", 1024, F32),
                     ("BT", 512, BF16), ("CT", 512, BF16), ("mergedT", 2048, BF16)]:
        dint(n, [w, T], dt)
    dint("Vd", [T, 1024], BF16)
    dint("dtraw", [T, 16], F32)
    dint("QaT", [8, 192, T], BF16)
    dint("KaT", [8, 128, T], BF16)
    dint("Va", [T, 1024], BF16)
    dint("Xtok", [T, 1024], F32)
    dint("Btok", [T, 512], BF16)
    dint("YsT", [2, 1024, T], F32)
    dint("YT", [4, 1024, T], BF16)
    dint("XS", [T, D], F32)
    env["identf"], env["ident"] = make_ident(f)
    consts(f, env)
    for bb in range(nb):
        env["c"] = cs[bb]
        for l in range(depth):
            xsrc = xins[bb] if l == 0 else env["XS"]
            phase_mods(f, env, l)
            phase_ln(f, env, l, xsrc)
            phase_inproj(f, env, l)
            phase_mla_prep(f, env, l)
            phase_attention(f, env, l, "mla")
            phase_diff_prep(f, env, l)
            phase_attention(f, env, l, "diff")
            phase_s5(f, env, l)
            phase_s5_glu(f, env, l)
            phase_ssd_prep(f, env, l)
            phase_ssd_scan(f, env, l)
            phase_ssd_fin(f, env, l)
            phase_merge(f, env, l)
            if l < depth - 1:
                phase_out(f, env, l, xsrc, env["XS"], 0)
            else:
                phase_out(f, env, l, xsrc, outs[bb], 256)
    f.barrier()
    return nc, f


def kernel(**inputs):
    nb = 4 // NCORES
    nc, f = build_program(nb=nb)
    C2, S2 = rope_tables_host()
    x = np.asarray(inputs["x"], np.float32)
    ctx = np.asarray(inputs["ctx"], np.float32)
    c = np.asarray(inputs["c"], np.float32)
    c_ctx = np.asarray(inputs["c_ctx"], np.float32)
    in_maps = []
    for core in range(NCORES):
        m = {n: np.ascontiguousarray(np.asarray(inputs[n], np.float32)) for n in WEIGHT_SHAPES}
        m["ropeC"] = C2
        m["ropeS"] = S2
        m["c_ctx"] = np.ascontiguousarray(c_ctx[None, :])
        for bb in range(nb):
            b = core * nb + bb
            m[f"x_all{bb}"] = np.ascontiguousarray(np.concatenate([ctx[b], x[b]], 0))
            m[f"c{bb}"] = np.ascontiguousarray(c[b:b + 1])
        in_maps.append(m)
    res = run_bass_kernel_spmd(nc, in_maps, core_ids=list(range(NCORES)))
    out = np.zeros((4, 4096, 2048), np.float32)
    for core in range(NCORES):
        for bb in range(nb):
            out[core * nb + bb] = res.results[core][f"out{bb}"]
    return out
```
